# Optimizing a Trainium2 kernel written in Bass

```python
import jax, jax.numpy as jnp
from jax import lax
import numpy as np

D_MODEL = 1024
BATCH = 32
SEQ = 2048
DEPTH = 2

GRID_W = 64
Q_BLOCK = 128
ROPE_THETA = 10000.0
NORM_EPS = 1e-6
MLA_HEADS = 8
MLA_Q_RANK = 384
MLA_KV_RANK = 256
MLA_NOPE_DIM = 64
MLA_ROPE_DIM = 32
MLA_V_DIM = 64
MLA_QK_DIM = MLA_NOPE_DIM + MLA_ROPE_DIM
MLA_OUT = MLA_HEADS * MLA_V_DIM
GQA_Q_HEADS = 8
GQA_KV_HEADS = 2
GQA_GROUP = GQA_Q_HEADS // GQA_KV_HEADS
GQA_HEAD_DIM = 64
GQA_OUT = GQA_Q_HEADS * GQA_HEAD_DIM
N_BRANCHES = 2
D_FF = 2816
N_MOD = 9
IN_SIZES = (MLA_Q_RANK, MLA_KV_RANK, MLA_ROPE_DIM,
            GQA_Q_HEADS * GQA_HEAD_DIM, GQA_KV_HEADS * GQA_HEAD_DIM, GQA_KV_HEADS * GQA_HEAD_DIM,
            N_BRANCHES * D_MODEL)
D_IN = sum(IN_SIZES)

kernel_name = 'hybrid_mla_gqa_axial_macaron_adaln_encoder'


def _rmsnorm(x, g):
    x32 = x.astype(jnp.float32)
    y = x32 * lax.rsqrt(jnp.mean(x32 * x32, axis=-1, keepdims=True) + NORM_EPS)
    return (y * g.astype(jnp.float32)).astype(x.dtype)


def _split_cols(z, sizes):
    out, start = [], 0
    for s in sizes:
        out.append(z[..., start:start + s])
        start += s
    return out


def _rope_tables(pos, dim):
    inv = ROPE_THETA ** (-jnp.arange(0, dim, 2, dtype=jnp.float32) / dim)
    ang = pos.astype(jnp.float32)[:, None] * inv[None, :]
    return jnp.cos(ang), jnp.sin(ang)


def _apply_rope(x, cos, sin):
    x32 = x.astype(jnp.float32)
    x1, x2 = jnp.split(x32, 2, axis=-1)
    c = cos[None, :, None, :]
    s = sin[None, :, None, :]
    return jnp.concatenate([x1 * c - x2 * s, x2 * c + x1 * s], axis=-1).astype(x.dtype)


def _axial_rope(x, row_cs, col_cs):
    half = x.shape[-1] // 2
    return jnp.concatenate([_apply_rope(x[..., :half], *row_cs),
                            _apply_rope(x[..., half:], *col_cs)], axis=-1)


def _block_attention(q, k, v, scale):
    B, S, Hk, G, Dq = q.shape
    nb = S // Q_BLOCK
    qb = q.reshape(B, nb, Q_BLOCK, Hk, G, Dq).transpose(1, 0, 2, 3, 4, 5)

    def one_block(q_blk):
        s = jnp.einsum('bqhgd,bshd->bhgqs', q_blk, k, preferred_element_type=jnp.float32) * scale
        p = jax.nn.softmax(s, axis=-1).astype(v.dtype)
        return jnp.einsum('bhgqs,bshd->bqhgd', p, v)

    o = lax.map(one_block, qb)
    return o.transpose(1, 0, 2, 3, 4, 5).reshape(B, S, Hk * G * v.shape[-1])


def _swiglu(x, w_in, w_out):
    a, b = jnp.split(x @ w_in, 2, axis=-1)
    return (jax.nn.silu(a) * b) @ w_out


def _modulate(xn, shift, scale):
    return xn * (1.0 + scale) + shift


def _mla(z_q, z_kv, z_kr, q_a_norm, w_uq, kv_a_norm, w_ukv, qk_q_norm, qk_k_norm, pos_cs):
    B, S, _ = z_q.shape
    q = (_rmsnorm(z_q, q_a_norm) @ w_uq).reshape(B, S, MLA_HEADS, MLA_QK_DIM)
    kv = (_rmsnorm(z_kv, kv_a_norm) @ w_ukv).reshape(B, S, MLA_HEADS, MLA_NOPE_DIM + MLA_V_DIM)
    k_nope, v = kv[..., :MLA_NOPE_DIM], kv[..., MLA_NOPE_DIM:]
    k_pe = jnp.broadcast_to(z_kr[:, :, None, :], (B, S, MLA_HEADS, MLA_ROPE_DIM))
    k = jnp.concatenate([k_nope, k_pe], axis=-1)
    q = _rmsnorm(q, qk_q_norm)
    k = _rmsnorm(k, qk_k_norm)
    q = jnp.concatenate([q[..., :MLA_NOPE_DIM], _apply_rope(q[..., MLA_NOPE_DIM:], *pos_cs)], axis=-1)
    k = jnp.concatenate([k[..., :MLA_NOPE_DIM], _apply_rope(k[..., MLA_NOPE_DIM:], *pos_cs)], axis=-1)
    return _block_attention(q[:, :, :, None, :], k, v, MLA_QK_DIM ** -0.5)


def _gqa_axial(z_q, z_k, z_v, q_norm, k_norm, row_cs, col_cs):
    B, S, _ = z_q.shape
    q = _rmsnorm(z_q.reshape(B, S, GQA_Q_HEADS, GQA_HEAD_DIM), q_norm)
    k = _rmsnorm(z_k.reshape(B, S, GQA_KV_HEADS, GQA_HEAD_DIM), k_norm)
    v = z_v.reshape(B, S, GQA_KV_HEADS, GQA_HEAD_DIM)
    q = _axial_rope(q, row_cs, col_cs).reshape(B, S, GQA_KV_HEADS, GQA_GROUP, GQA_HEAD_DIM)
    k = _axial_rope(k, row_cs, col_cs)
    return _block_attention(q, k, v, GQA_HEAD_DIM ** -0.5)


def setup_inputs(seed: int = 0) -> dict:
    key = jax.random.key(seed)
    ks = jax.random.split(key, 24)
    f32 = jnp.float32

    def w(k, shape, fan_in):
        return jax.random.normal(k, shape, f32) * (fan_in ** -0.5)

    def gain(k, shape):
        return 1.0 + 0.02 * jax.random.normal(k, shape, f32)

    L = DEPTH
    return {
        'x': jax.random.normal(ks[0], (BATCH, SEQ, D_MODEL), f32),
        'c': jax.random.normal(ks[1], (BATCH, D_MODEL), f32),
        'w_ada': w(ks[2], (L, D_MODEL, N_MOD * D_MODEL), D_MODEL),
        'b_ada': 0.02 * jax.random.normal(ks[3], (L, N_MOD * D_MODEL), f32),
        'norm_ffn1': gain(ks[4], (L, D_MODEL)),
        'w_ffn1_in': w(ks[5], (L, D_MODEL, 2 * D_FF), D_MODEL),
        'w_ffn1_out': w(ks[6], (L, D_FF, D_MODEL), D_FF),
        'norm_mix': gain(ks[7], (L, D_MODEL)),
        'w_in': w(ks[8], (L, D_MODEL, D_IN), D_MODEL),
        'mla_q_a_norm': gain(ks[9], (L, MLA_Q_RANK)),
        'mla_w_uq': w(ks[10], (L, MLA_Q_RANK, MLA_HEADS * MLA_QK_DIM), MLA_Q_RANK),
        'mla_kv_a_norm': gain(ks[11], (L, MLA_KV_RANK)),
        'mla_w_ukv': w(ks[12], (L, MLA_KV_RANK, MLA_HEADS * (MLA_NOPE_DIM + MLA_V_DIM)), MLA_KV_RANK),
        'mla_qk_q_norm': gain(ks[13], (L, MLA_QK_DIM)),
        'mla_qk_k_norm': gain(ks[14], (L, MLA_QK_DIM)),
        'gqa_q_norm': gain(ks[15], (L, GQA_HEAD_DIM)),
        'gqa_k_norm': gain(ks[16], (L, GQA_HEAD_DIM)),
        'w_branch_mla': w(ks[17], (L, MLA_OUT, D_MODEL), MLA_OUT),
        'w_branch_gqa': w(ks[18], (L, GQA_OUT, D_MODEL), GQA_OUT),
        'w_out': w(ks[19], (L, D_MODEL, D_MODEL), D_MODEL),
        'norm_ffn2': gain(ks[20], (L, D_MODEL)),
        'w_ffn2_in': w(ks[21], (L, D_MODEL, 2 * D_FF), D_MODEL),
        'w_ffn2_out': w(ks[22], (L, D_FF, D_MODEL), D_FF),
    }


def reference(x, c, w_ada, b_ada, norm_ffn1, w_ffn1_in, w_ffn1_out, norm_mix, w_in,
              mla_q_a_norm, mla_w_uq, mla_kv_a_norm, mla_w_ukv, mla_qk_q_norm, mla_qk_k_norm,
              gqa_q_norm, gqa_k_norm, w_branch_mla, w_branch_gqa, w_out,
              norm_ffn2, w_ffn2_in, w_ffn2_out):
    B, S, D = x.shape
    rows = S // GRID_W
    t = jnp.arange(S)
    row = jnp.repeat(jnp.arange(rows), GRID_W)
    col = jnp.tile(jnp.arange(GRID_W), rows)
    pos_cs = _rope_tables(t, MLA_ROPE_DIM)
    row_cs = _rope_tables(row, GQA_HEAD_DIM // 2)
    col_cs = _rope_tables(col, GQA_HEAD_DIM // 2)
    c_act = jax.nn.silu(c)

    h = x
    for l in range(DEPTH):
        mod = (c_act @ w_ada[l] + b_ada[l]).reshape(B, N_MOD, 1, D)
        sh1, sc1, g1, sh2, sc2, g2, sh3, sc3, g3 = [mod[:, i] for i in range(N_MOD)]

        u = _modulate(_rmsnorm(h, norm_ffn1[l]), sh1, sc1)
        h = h + 0.5 * g1 * _swiglu(u, w_ffn1_in[l], w_ffn1_out[l])

        u = _modulate(_rmsnorm(h, norm_mix[l]), sh2, sc2)
        z_q, z_kv, z_kr, z_gq, z_gk, z_gv, z_gate = _split_cols(u @ w_in[l], IN_SIZES)
        o_mla = _mla(z_q, z_kv, z_kr, mla_q_a_norm[l], mla_w_uq[l], mla_kv_a_norm[l], mla_w_ukv[l],
                     mla_qk_q_norm[l], mla_qk_k_norm[l], pos_cs)
        o_gqa = _gqa_axial(z_gq, z_gk, z_gv, gqa_q_norm[l], gqa_k_norm[l], row_cs, col_cs)
        gate_mla, gate_gqa = jnp.split(jax.nn.sigmoid(z_gate), N_BRANCHES, axis=-1)
        merged = gate_mla * (o_mla @ w_branch_mla[l]) + gate_gqa * (o_gqa @ w_branch_gqa[l])
        h = h + g2 * (merged @ w_out[l])

        u = _modulate(_rmsnorm(h, norm_ffn2[l]), sh3, sc3)
        h = h + 0.5 * g3 * _swiglu(u, w_ffn2_in[l], w_ffn2_out[l])
    return h
```

```python
import numpy as np
from contextlib import ExitStack
import concourse.bass as bass
import concourse.mybir as mybir
from concourse.bass_utils import run_bass_kernel_spmd

F32 = mybir.dt.float32
BF16 = mybir.dt.bfloat16
AF = mybir.ActivationFunctionType
ALU = mybir.AluOpType
AX = mybir.AxisListType

NCORES = 8
L = 2
D = 1024
DC = 8
T = 2048
TT = 512
NT = 4
FF = 2816
FC = 22
DIN = 3488
EPS = 1e-6
SAME_ENG_RAW_SYNC = True


class Tile:
    __slots__ = ("name", "ap", "lo", "hi", "plo", "phi", "lw", "rd", "ov", "ps")

    def __init__(self, name, ap, lo, hi, plo, phi):
        self.name, self.ap, self.lo, self.hi, self.plo, self.phi = name, ap, lo, hi, plo, phi
        self.ps = False
        self.lw = None
        self.rd = {}
        self.ov = []


class Op:
    __slots__ = ("idx", "eng", "fn", "deps", "dma_key", "dma_n", "sig_needed", "sig", "group")

    def __init__(self, idx, eng, fn, deps, dma_key, group):
        self.idx, self.eng, self.fn, self.deps, self.dma_key, self.group = idx, eng, fn, deps, dma_key, group
        self.dma_n = 0
        self.sig_needed = False
        self.sig = 0


class Prog:
    def __init__(self):
        self.ops = []
        self.tiles = []
        self.dma_cnt = {}
        self.space_tiles = {}

    def tile(self, name, ap, space, lo, hi, plo=0, phi=128):
        t = Tile(name, ap, lo, hi, plo, phi)
        t.ps = (space == "ps")
        lst = self.space_tiles.setdefault(space, [])
        for o in lst:
            if o.lo < hi and lo < o.hi and o.plo < phi and plo < o.phi:
                t.ov.append(o)
                o.ov.append(t)
        lst.append(t)
        return t

    def _need(self, a, eng, is_dma, kind):
        if a.dma_key is not None:
            return True
        if (not is_dma) and a.eng == eng:
            return SAME_ENG_RAW_SYNC and eng != "pe"
        return True

    def add(self, eng, fn, reads=(), writes=(), dma_key=None, group=False):
        idx = len(self.ops)
        deps = {}
        for t in reads:
            if t.lw is not None:
                deps[t.lw] = 0
            for x in t.ov:
                if x.lw is not None:
                    deps[x.lw] = 0
            if t.ps:
                for rk_, r in t.rd.items():
                    if rk_ != eng and r not in deps:
                        deps[r] = 2
        for t in writes:
            for x in [t] + t.ov:
                if x.lw is not None and x.lw not in deps:
                    deps[x.lw] = 1
                for r in x.rd.values():
                    if r not in deps:
                        deps[r] = 2
        is_dma = dma_key is not None
        fdeps = []
        latest = {}
        for a_idx, kind in deps.items():
            a = self.ops[a_idx]
            if self._need(a, eng, is_dma, kind):
                if a.dma_key is None:
                    if a.eng not in latest or latest[a.eng] < a_idx:
                        latest[a.eng] = a_idx
                else:
                    fdeps.append(a_idx)
        for a_idx in latest.values():
            fdeps.append(a_idx)
            self.ops[a_idx].sig_needed = True
        op = Op(idx, eng, fn, fdeps, dma_key, group)
        if is_dma:
            self.dma_cnt[dma_key] = self.dma_cnt.get(dma_key, 0) + 1
            op.dma_n = self.dma_cnt[dma_key]
        rk = ("d", dma_key) if is_dma else eng
        for t in reads:
            t.rd[rk] = idx
        for t in writes:
            t.lw = idx
            t.rd = {}
        self.ops.append(op)
        return op

    def emit(self, nc, es):
        engs = ["pe", "act", "dve", "pool", "sp"]
        cnt = {e: 0 for e in engs}
        for op in self.ops:
            if op.dma_key is None and op.sig_needed:
                cnt[op.eng] += 1
                op.sig = cnt[op.eng]
        esem = {e: es.enter_context(nc.semaphore("sem_" + e)) for e in engs}
        dsem = {k: es.enter_context(nc.semaphore("dsem_%d" % i)) for i, k in enumerate(sorted(self.dma_cnt, key=str))}
        block = es.enter_context(nc.Block())
        ops = self.ops
        dma_cnt = self.dma_cnt

        def stream(ename):
            def body(eng):
                waited = {}
                for op in ops:
                    if op.eng != ename:
                        continue
                    need = {}
                    for a_idx in op.deps:
                        a = ops[a_idx]
                        if a.dma_key is not None:
                            sem = dsem[a.dma_key]
                            val = 16 * (dma_cnt[a.dma_key] if a.group else a.dma_n)
                        else:
                            sem = esem[a.eng]
                            val = a.sig
                        k = id(sem)
                        if k not in need or need[k][1] < val:
                            need[k] = (sem, val)
                    for k, (sem, val) in need.items():
                        if waited.get(k, 0) < val:
                            eng.wait_ge(sem, val)
                            waited[k] = val
                    inst = op.fn(eng)
                    if op.dma_key is not None:
                        inst.then_inc(dsem[op.dma_key], 16)
                    elif op.sig_needed:
                        inst.then_inc(esem[ename], 1)
            return body

        block.tensor(stream("pe"))
        block.scalar(stream("act"))
        block.vector(stream("dve"))
        block.gpsimd(stream("pool"))
        block.sync(stream("sp"))


def build_program(nseq, depth=L, stop_stage=0):
    nc = bass.Bass("TRN2", target_bir_lowering=False)
    P = Prog()

    def din(name, shape):
        return nc.dram_tensor(name, list(shape), F32, kind="ExternalInput").ap()

    x_d = din("x", [nseq, T, D])
    cT_d = din("cT", [D, nseq])
    w_ada_d = din("w_ada", [L, D, 9 * D])
    b_adaT_d = din("b_adaT", [L, 128, 72])
    norm3_d = din("norm3", [128, L * 3 * 8])
    w_f1i_d = din("w_ffn1_in", [L, D, 2 * FF])
    w_f1o_d = din("w_ffn1_out", [L, FF, D])
    w_f2i_d = din("w_ffn2_in", [L, D, 2 * FF])
    w_f2o_d = din("w_ffn2_out", [L, FF, D])
    w_in_d = din("w_in", [L, D, DIN])
    w_gqsw_d = din("w_gq_sw", [L, D, 512])
    w_gkrep_d = din("w_gk_rep", [L, D, 256])
    w_gkswrep_d = din("w_gk_sw_rep", [L, D, 256])
    w_krpad_d = din("w_kr_pad", [L, D, 96])
    w_krswpad_d = din("w_kr_sw_pad", [L, D, 96])
    w_uq_d = din("mla_w_uq", [L, 384, 768])
    w_uqsw_d = din("w_uq_sw", [L, 384, 768])
    w_ukv_d = din("mla_w_ukv", [L, 256, 1024])
    w_bm_d = din("w_branch_mla", [L, 512, D])
    w_bg_d = din("w_branch_gqa", [L, 512, D])
    w_out_d = din("w_out", [L, D, D])
    NG = L * 3 + L * 2 + 8 * L
    gcols_d = din("gcols", [128, NG])
    grow_d = din("grow", [1, L * 4 * 96])
    tabs_d = din("tabs", [4, 128, T])
    ident_d = din("ident", [128, 128])
    out_d = nc.dram_tensor("out", [nseq, T, D], F32, kind="ExternalOutput").ap()

    es = ExitStack()
    with es:
        OFF_H = 0
        OFF_UO = OFF_H + 65536
        OFF_OG = OFF_UO + 32768
        OFF_R1 = OFF_OG + 16384
        OFF_T = OFF_R1 + 34816
        OFF_W = OFF_T + 16384
        NWS = 3
        OFF_S = OFF_W + NWS * 4096
        OFF_P = OFF_S + 16384
        OFF_C = OFF_P + 4096
        ARENA = OFF_C + 8192
        arena = es.enter_context(nc.sbuf_tensor("arena", [128, ARENA // 2], BF16))
        psall = es.enter_context(nc.psum_tensor("psall", [128, 4096], F32))

        def view(off, nbytes, dt):
            a = arena[:, off // 2:(off + nbytes) // 2]
            if dt == F32:
                a = a.bitcast(F32)
            return a

        def mk(name, off, nbytes, dt, plo=0, phi=128, shape=None):
            a = view(off, nbytes, dt)
            if shape is not None:
                if len(shape) == 1:
                    a = a.rearrange("p (a b) -> p a b", a=shape[0])
                elif len(shape) == 2:
                    a = a.rearrange("p (a b c) -> p a b c", a=shape[0], b=shape[1])
            if plo != 0 or phi != 128:
                a = a[plo:phi]
            return P.tile(name, a, "sb", off, off + nbytes, plo, phi)

        PS = [P.tile("ps%d" % i, psall[:, i * 512:(i + 1) * 512], "ps", i * 2048, (i + 1) * 2048) for i in range(8)]

        hT = [[mk("h", OFF_H + (dc * T + tt * TT) * 4, TT * 4, F32) for tt in range(NT)] for dc in range(DC)]
        uT = [[mk("u", OFF_UO + (dc * T + tt * TT) * 2, TT * 2, BF16) for tt in range(NT)] for dc in range(DC)]
        om = [[[mk("om", OFF_UO + (p * T + tt * TT) * 2, TT * 2, BF16, h * 64, h * 64 + 64) for h in range(2)]
               for tt in range(NT)] for p in range(4)]
        og = [[[mk("og", OFF_OG + (p * T + tt * TT) * 2, TT * 2, BF16, h * 64, h * 64 + 64) for h in range(2)]
               for tt in range(NT)] for p in range(4)]
        OFF_PH = OFF_UO + 16384
        kTh = [mk("kTh", OFF_PH + tt * TT * 2, TT * 2, BF16, 0, 96) for tt in range(NT)]
        qTh = [mk("qTh", OFF_PH + 4096 + tt * TT * 2, TT * 2, BF16, 0, 96) for tt in range(NT)]
        kThF = [mk("kThF", OFF_PH + tt * TT * 2, TT * 2, BF16) for tt in range(NT)]
        qThF = [mk("qThF", OFF_PH + 4096 + tt * TT * 2, TT * 2, BF16) for tt in range(NT)]
        Vh = mk("Vh", OFF_PH + 8192, 16 * 192 * 2, BF16, shape=(16, 3))
        gq = [[mk("gq", OFF_R1 + (p * T + tt * TT) * 2, TT * 2, BF16) for tt in range(NT)] for p in range(4)]
        gkz = [[[mk("gkz", OFF_R1 + 16384 + ((g * 2 + hf) * T + tt * TT) * 2, TT * 2, BF16) for tt in range(NT)]
                for hf in range(2)] for g in range(2)]
        gv = [mk("gv", OFF_T + c4 * 2560, 2560, BF16, shape=(4, 5)) for c4 in range(4)]
        qn = [[mk("qn", OFF_R1 + (m * T + tt * TT) * 2, TT * 2, BF16) for tt in range(NT)] for m in range(3)]
        kvn = [[mk("kvn", OFF_R1 + 12288 + (m * T + tt * TT) * 2, TT * 2, BF16) for tt in range(NT)] for m in range(2)]
        gbn = [mk("gbn", OFF_R1 + 20480 + tt * TT * 4, TT * 4, F32, 0, 64) for tt in range(NT)]
        gbk = [mk("gbk", OFF_R1 + 20480 + tt * TT * 4, TT * 4, F32, 64, 96) for tt in range(NT)]
        sqn = [mk("sqn", OFF_R1 + 28672 + tt * TT * 2, TT * 2, BF16, 0, 64) for tt in range(NT)]
        sqk = [mk("sqk", OFF_R1 + 28672 + tt * TT * 2, TT * 2, BF16, 64, 96) for tt in range(NT)]
        yT = [[mk("y", OFF_R1 + (j * T + tt * TT) * 2, TT * 2, BF16) for tt in range(NT)] for j in range(6)]
        upT = [[mk("up", OFF_R1 + (dc * 1024 + t2 * TT) * 2, TT * 2, BF16) for t2 in range(2)] for dc in range(DC)]
        mT = [[mk("mT", OFF_R1 + 16384 + (dc * 1024 + t2 * TT) * 2, TT * 2, BF16) for t2 in range(2)] for dc in range(DC)]
        tabC = mk("tabC", OFF_T, 8192, F32)
        tabS = mk("tabS", OFF_T + 8192, 8192, F32)
        wout = [mk("wout", OFF_T + s * 2048, 2048, BF16) for s in range(6)]
        xs = [mk("xs", OFF_T + s * 4096, 4096, F32) for s in range(2)]
        ys = [mk("ys", OFF_T + 8192 + s * 4096, 4096, F32) for s in range(2)]
        wslot = [mk("w", OFF_W + s * 4096, 4096, BF16) for s in range(NWS)]
        rstd_r = [mk("rstd", OFF_S + s * 2048, 2048, F32) for s in range(2)]
        tA_r = [mk("tA", OFF_S + 4096 + s * 2048, 2048, F32) for s in range(2)]
        tB_r = [mk("tB", OFF_S + 8192 + s * 2048, 2048, F32) for s in range(2)]
        sq_r = [mk("sq", OFF_S + 12288 + s * 1024, 1024, BF16) for s in range(2)]
        rinv = mk("rinv", OFF_S + 14336, 2048, F32)
        PT = [mk("pt", OFF_P + s * 2048, 2048, BF16) for s in range(2)]
        co = [OFF_C]

        def cmk(name, nbytes, dt, shape=None):
            nb = (nbytes + 31) // 32 * 32
            t = mk(name, co[0], nb, dt, shape=None)
            co[0] += nb
            assert co[0] <= ARENA
            return t

        ident = cmk("ident", 512, F32)
        ones_bf = cmk("ones", 256, BF16)
        blk_bf = cmk("blk", 256, BF16)
        ones_f = cmk("ones_f", 512, F32)
        modT = cmk("modT", L * 72 * 4 * 4, F32)
        badaT = cmk("badaT", L * 72 * 4, F32)
        norm3 = cmk("norm3", L * 3 * 8 * 4, F32)
        gcols = cmk("gcols", NG * 4, F32)
        growt = mk("grow", OFF_T, L * 4 * 96 * 4, F32)
        gmax = cmk("gmax", L * 4 * 4, F32)
        nbias = cmk("nbias", L * 2 * 4, F32)
        epst = cmk("eps", 4, F32)
        cTt = cmk("cT", 8 * nseq * 4, F32)
        cact = cmk("cact", 8 * nseq * 2, BF16)
        gsv = cmk("gsv", 3 * 8 * 4, F32)
        hgv = cmk("hgv", 2 * 8 * 4, F32)

        rings = {}

        def ring(name, lst):
            i = rings.get(name, 0)
            rings[name] = i + 1
            return lst[i % len(lst)]

        def bank(name, ids):
            return PS[ring("bank_" + name, ids)]

        def mm(out_ap, lhsT, rhs, start, stop, reads, writes):
            P.add("pe", lambda e: e.matmul(out_ap, lhsT, rhs, start=start, stop=stop), reads, writes)

        def act(out_ap, in_ap, func, reads, writes, bias=None, scale=None):
            kw = {}
            if bias is not None:
                kw["bias"] = bias
            if scale is not None:
                kw["scale"] = scale
            P.add("act", lambda e: e.activation(out=out_ap, in_=in_ap, func=func, **kw), reads, writes)

        def stt(out_ap, in0, scalar, in1, op0, op1, reads, writes):
            P.add("dve", lambda e: e.scalar_tensor_tensor(out_ap, in0, scalar, in1, op0, op1), reads, writes)

        def tt_(out_ap, in0, in1, op, reads, writes):
            P.add("dve", lambda e: e.tensor_tensor(out_ap, in0, in1, op), reads, writes)

        def dcopy(out_ap, in_ap, reads, writes):
            P.add("dve", lambda e: e.tensor_copy(out_ap, in_ap), reads, writes)

        def dma_sp(out_ap, in_ap, reads, writes, key, group=False, slow=False):
            if slow:
                P.add("sp", lambda e: e.dma_start(out=out_ap, in_=in_ap, allow_slow_non_contiguous=True),
                      reads, writes, dma_key=key, group=group)
            else:
                P.add("sp", lambda e: e.dma_start(out=out_ap, in_=in_ap), reads, writes, dma_key=key, group=group)

        def dma_pool(out_ap, in_ap, reads, writes, key):
            P.add("pool", lambda e: e.dma_start(out=out_ap, in_=in_ap), reads, writes, dma_key=key)

        wctr = [0]

        def wload(pieces, kcs):
            s = wctr[0] % NWS
            wctr[0] += 1
            slot = wslot[s]
            tot = sum(n for _, n in pieces)
            assert kcs * tot <= 2048
            v = slot.ap[:, 0:kcs * tot].rearrange("p (k c) -> p k c", k=kcs)
            c0 = 0
            for src, n in pieces:
                srcv = src.rearrange("(k p) c -> p k c", p=128)
                dma_pool(v[:, :, c0:c0 + n], srcv, [], [slot], ("w", s))
                c0 += n
            return slot, v

        CK = "const"
        dma_sp(ident.ap[:, 0:128], ident_d, [], [ident], CK, group=True)
        dma_sp(badaT.ap[:, 0:L * 72].rearrange("p (l m) -> p l m", l=L), b_adaT_d.rearrange("l p m -> p l m"),
               [], [badaT], CK, group=True)
        dma_sp(norm3.ap[:, 0:L * 24], norm3_d, [], [norm3], CK, group=True)
        dma_sp(gcols.ap[:, 0:NG], gcols_d, [], [gcols], CK, group=True)
        dma_sp(growt.ap[0:1, 0:L * 4 * 96], grow_d, [], [growt], CK, group=True)
        dma_sp(cTt.ap[:, 0:8 * nseq].rearrange("p (k b) -> p k b", k=8), cT_d.rearrange("(k p) b -> p k b", p=128),
               [], [cTt], CK, group=True, slow=True)
        P.add("dve", lambda e: e.memset(ones_bf.ap[:, 0:128], 1.0), [], [ones_bf])
        P.add("dve", lambda e: e.memset(ones_f.ap[:, 0:128], 1.0), [], [ones_f])
        P.add("dve", lambda e: e.memset(blk_bf.ap[:, 0:128], 0.0), [], [blk_bf])
        P.add("dve", lambda e: e.memset(blk_bf.ap[0:64, 0:64], 1.0), [], [blk_bf])
        P.add("dve", lambda e: e.memset(blk_bf.ap[64:128, 64:128], 1.0), [], [blk_bf])
        P.add("dve", lambda e: e.memset(epst.ap[:, 0:1], EPS), [], [epst])

        gr = growt.ap[0:1, 0:L * 4 * 96].rearrange("p (a b) -> p a b", b=96)
        P.add("dve", lambda e: e.tensor_reduce(gmax.ap[0:1, 0:L * 4], gr, AX.X, ALU.max, apply_absolute_value=True),
              [growt], [gmax])
        for l in range(L):
            for mi, dd in ((0, 96.0), (1, 64.0)):
                a = gmax.ap[0:1, l * 4 + 2 * mi:l * 4 + 2 * mi + 1]
                b = gmax.ap[0:1, l * 4 + 2 * mi + 1:l * 4 + 2 * mi + 2]
                P.add("dve", (lambda a=a, b=b, dd=dd: lambda e: e.scalar_tensor_tensor(
                    a, a, -float(np.sqrt(dd)), b, ALU.mult, ALU.mult))(), [gmax], [gmax])
        bk = PS[7]
        mm(bk.ap[:, 0:L * 4], ones_f.ap[0:1, 0:128], gmax.ap[0:1, 0:L * 4], True, True, [ones_f, gmax], [bk])
        for l in range(L):
            for mi in range(2):
                src = bk.ap[:, l * 4 + 2 * mi:l * 4 + 2 * mi + 1]
                dst = nbias.ap[:, l * 2 + mi:l * 2 + mi + 1]
                dcopy(dst, src, [bk], [nbias])

        act(cact.ap[:, 0:8 * nseq], cTt.ap[:, 0:8 * nseq], AF.Silu, [cTt], [cact])
        cactv = cact.ap[:, 0:8 * nseq].rearrange("p (k b) -> p k b", k=8)
        modv = modT.ap[:, 0:L * 72 * 4].rearrange("p (l m b) -> p l m b", l=L, m=72)
        badv = badaT.ap[:, 0:L * 72].rearrange("p (l m) -> p l m", l=L)
        for l in range(depth):
            mb = PS[6]
            for s in range(36):
                slot, v = wload([(w_ada_d[l][:, s * 256:(s + 1) * 256], 256)], 8)
                for mloc in range(2):
                    m = s * 2 + mloc
                    for kc in range(8):
                        mm(mb.ap[:, m * nseq:(m + 1) * nseq], v[:, kc, mloc * 128:(mloc + 1) * 128], cactv[:, kc, :],
                           kc == 0, kc == 7, [slot, cact], [mb])
            mbv = mb.ap[:, 0:72 * nseq].rearrange("p (m b) -> p m b", b=nseq)
            for b in range(nseq):
                tt_(modv[:, l, :, b], mbv[:, :, b], badv[:, l, :], ALU.add, [mb, badaT], [modT])

        def modcol(l, i, dc, b):
            return modv[:, l, i * 8 + dc, b:b + 1]

        n3v = norm3.ap[:, 0:L * 24].rearrange("p (l i d) -> p l i d", l=L, i=3)
        gsvv = gsv.ap[:, 0:24].rearrange("p (i d) -> p i d", i=3)
        hgvv = hgv.ap[:, 0:16].rearrange("p (i d) -> p i d", i=2)

        def gc(col):
            return gcols.ap[:, col:col + 1]

        GC_QA = 0
        GC_KVA = L * 3
        GC_X = L * 3 + L * 2

        def norm_mod(l, b, i, tts, dst):
            for k, tt in enumerate(tts):
                sb = bank("stat", [6, 7])
                for dc in range(DC):
                    sq = ring("sq", sq_r)
                    act(sq.ap, hT[dc][tt].ap, AF.Square, [hT[dc][tt]], [sq])
                    mm(sb.ap, ones_bf.ap[:, 0:128], sq.ap, dc == 0, dc == DC - 1, [ones_bf, sq], [sb])
                rs = ring("rstd", rstd_r)
                act(rs.ap, sb.ap, AF.Ln, [sb, epst], [rs], bias=epst.ap[:, 0:1], scale=1.0 / D)
                act(rs.ap, rs.ap, AF.Exp, [rs], [rs], scale=-0.5)
                for dc in range(DC):
                    ta = ring("tA", tA_r)
                    stt(ta.ap, hT[dc][tt].ap, gsvv[:, i, dc:dc + 1], rs.ap, ALU.mult, ALU.mult,
                        [hT[dc][tt], gsv, rs], [ta])
                    act(dst[dc][k].ap, ta.ap, AF.Identity, [ta, modT], [dst[dc][k]], bias=modcol(l, 3 * i, dc, b), scale=1.0)

        def ffn(l, b, which):
            w_in_dram = (w_f1i_d if which == 0 else w_f2i_d)[l]
            w_out_dram = (w_f1o_d if which == 0 else w_f2o_d)[l]
            i = 0 if which == 0 else 2
            hi = 0 if which == 0 else 1
            norm_mod(l, b, i, list(range(NT)), uT)
            quarters = [list(range(0, 6)), list(range(6, 12)), list(range(12, 17)), list(range(17, 22))]
            for chunks in quarters:
                for jj, j in enumerate(chunks):
                    slot, v = wload([(w_in_dram[:, j * 128:(j + 1) * 128], 128),
                                     (w_in_dram[:, FF + j * 128:FF + (j + 1) * 128], 128)], 8)
                    if jj == min(2, len(chunks) - 1):
                        for j2i, j2 in enumerate(chunks):
                            dma_pool(wout[j2i].ap, w_out_dram[j2 * 128:(j2 + 1) * 128, :], [], [wout[j2i]], ("wo", j2i))
                    for tt in range(NT):
                        pa = bank("A", [0, 1])
                        pb = bank("B", [2, 3])
                        for kc in range(8):
                            mm(pa.ap, v[:, kc, 0:128], uT[kc][tt].ap, kc == 0, kc == 7, [slot, uT[kc][tt]], [pa])
                        for kc in range(8):
                            mm(pb.ap, v[:, kc, 128:256], uT[kc][tt].ap, kc == 0, kc == 7, [slot, uT[kc][tt]], [pb])
                        tb = ring("tB", tB_r)
                        act(tb.ap, pa.ap, AF.Silu, [pa], [tb])
                        tt_(yT[jj][tt].ap, tb.ap, pb.ap, ALU.mult, [tb, pb], [yT[jj][tt]])
                for tt in range(NT):
                    for dc in range(DC):
                        po = bank("O", [4, 5])
                        for jj in range(len(chunks)):
                            mm(po.ap, wout[jj].ap[:, dc * 128:(dc + 1) * 128], yT[jj][tt].ap, jj == 0, jj == len(chunks) - 1,
                               [wout[jj], yT[jj][tt]], [po])
                        stt(hT[dc][tt].ap, po.ap, hgvv[:, hi, dc:dc + 1], hT[dc][tt].ap, ALU.mult, ALU.add,
                            [po, hgv, hT[dc][tt]], [hT[dc][tt]])

        def rstd_from(sb_ap, np_, dim, reads):
            rs = ring("rstd", rstd_r)
            act(rs.ap[0:np_], sb_ap, AF.Ln, reads + [epst], [rs], bias=epst.ap[0:np_, 0:1], scale=1.0 / dim)
            act(rs.ap[0:np_], rs.ap[0:np_], AF.Exp, [rs], [rs], scale=-0.5)
            return rs

        def finalize_rope(pz, pzs, np_, ones_ap, dim, gcol, gswcol, tt, dst, dst2=None):
            sq = ring("sq", sq_r)
            act(sq.ap[0:np_], pz.ap[0:np_], AF.Square, [pz], [sq])
            sb = bank("stat", [6, 7])
            mm(sb.ap[0:np_], ones_ap, sq.ap[0:np_], True, True, [ones_bf, blk_bf, sq], [sb])
            rs = rstd_from(sb.ap[0:np_], np_, dim, [sb])
            ta = ring("tA", tA_r)
            tb = ring("tB", tB_r)
            cs = slice(tt * TT, (tt + 1) * TT)
            stt(ta.ap[0:np_], pz.ap[0:np_], gc(gcol)[0:np_], tabC.ap[0:np_, cs], ALU.mult, ALU.mult, [pz, gcols, tabC], [ta])
            stt(tb.ap[0:np_], pzs.ap[0:np_], gc(gswcol)[0:np_], tabS.ap[0:np_, cs], ALU.mult, ALU.mult, [pzs, gcols, tabS], [tb])
            tt_(ta.ap[0:np_], ta.ap[0:np_], tb.ap[0:np_], ALU.add, [ta, tb], [ta])
            if dst2 is None:
                tt_(dst.ap[0:np_], ta.ap[0:np_], rs.ap[0:np_], ALU.mult, [ta, rs], [dst])
            else:
                tt_(dst.ap[0:64], ta.ap[0:64], rs.ap[0:64], ALU.mult, [ta, rs], [dst])
                tt_(dst2.ap[64:128], ta.ap[64:128], rs.ap[64:128], ALU.mult, [ta, rs], [dst2])

        def attention(kt_tiles, qt_tiles, prow, dk, v_of_chunk, scale, nb_ap, o_tiles, half):
            for tt in range(NT):
                po = bank("O", [4, 5])
                pairs = {}
                pts = {}

                def s_group(g):
                    pr = ring("spair", [0, 1])
                    pairs[g] = pr
                    for j in range(2):
                        c = 2 * g + j
                        sbk = PS[2 * pr + j]
                        kt = kt_tiles[c // 4]
                        mm(sbk.ap, kt.ap[prow, (c % 4) * 128:(c % 4 + 1) * 128], qt_tiles[tt].ap[prow, :], True, True,
                           [kt, qt_tiles[tt]], [sbk])
                    pt = ring("pt", PT)
                    pts[g] = pt
                    act(pt.ap, psall[:, pr * 1024:(pr + 1) * 1024], AF.Exp, [PS[2 * pr], PS[2 * pr + 1], nbias], [pt],
                        bias=nb_ap, scale=scale)

                def pv_group(g):
                    for j in range(2):
                        c = 2 * g + j
                        vt, vap = v_of_chunk(c)
                        mm(po.ap, vap, pts[g].ap[:, j * 512:(j + 1) * 512], c == 0, c == 15, [vt, pts[g]], [po])

                s_group(0)
                for g in range(8):
                    if g + 1 < 8:
                        s_group(g + 1)
                    pv_group(g)
                orow = slice(half * 64, half * 64 + 64)
                srow = slice((1 - half) * 64, (1 - half) * 64 + 64)
                P.add("dve", (lambda a=rinv.ap[srow], b=po.ap[srow]: lambda e: e.reciprocal(a, b))(), [po], [rinv])
                tt_(o_tiles[tt][half].ap, po.ap[orow], rinv.ap[srow], ALU.mult, [po, rinv], [o_tiles[tt][half]])

        def mixer(l, b):
            wi = w_in_d[l]
            gx = GC_X + 8 * l
            norm_mod(l, b, 1, list(range(NT)), uT)
            dma_sp(tabC.ap, tabs_d[0], [], [tabC], "tabC")
            dma_sp(tabS.ap, tabs_d[1], [], [tabS], "tabS")
            for g in range(2):
                for hf in range(2):
                    for tt in range(NT):
                        zr = slice((1 - hf) * 64, (1 - hf) * 64 + 64)
                        P.add("dve", (lambda a=gkz[g][hf][tt].ap[zr]: lambda e: e.memset(a, 0.0))(), [], [gkz[g][hf][tt]])
            for g in range(2):
                slot, v = wload([(w_gkrep_d[l][:, g * 128:(g + 1) * 128], 128), (w_gkswrep_d[l][:, g * 128:(g + 1) * 128], 128)], 8)
                for tt in range(NT):
                    pz = bank("Z", [0, 1, 2])
                    pzs = bank("Zs", [3, 4, 5])
                    for kc in range(8):
                        mm(pz.ap, v[:, kc, 0:128], uT[kc][tt].ap, kc == 0, kc == 7, [slot, uT[kc][tt]], [pz])
                    for kc in range(8):
                        mm(pzs.ap, v[:, kc, 128:256], uT[kc][tt].ap, kc == 0, kc == 7, [slot, uT[kc][tt]], [pzs])
                    finalize_rope(pz, pzs, 128, blk_bf.ap[:, 0:128], 64, gx + 6, gx + 7, tt, gkz[g][0][tt], gkz[g][1][tt])
            for p in range(4):
                slot, v = wload([(wi[:, 672 + p * 128:672 + (p + 1) * 128], 128), (w_gqsw_d[l][:, p * 128:(p + 1) * 128], 128)], 8)
                for tt in range(NT):
                    pz = bank("Z", [0, 1, 2])
                    pzs = bank("Zs", [3, 4, 5])
                    for kc in range(8):
                        mm(pz.ap, v[:, kc, 0:128], uT[kc][tt].ap, kc == 0, kc == 7, [slot, uT[kc][tt]], [pz])
                    for kc in range(8):
                        mm(pzs.ap, v[:, kc, 128:256], uT[kc][tt].ap, kc == 0, kc == 7, [slot, uT[kc][tt]], [pzs])
                    finalize_rope(pz, pzs, 128, blk_bf.ap[:, 0:128], 64, gx + 4, gx + 5, tt, gq[p][tt])
            for c4 in range(4):
                for blkk in (0, 2, 4):
                    P.add("dve", (lambda a=gv[c4].ap[:, :, blkk, :]: lambda e: e.memset(a, 1.0))(), [], [gv[c4]])
            slot, v = wload([(wi[:, 1312:1440], 128)], 8)
            for c4 in range(4):
                pv = bank("Z", [0, 1, 2])
                for cc in range(4):
                    for kc in range(8):
                        mm(pv.ap[:, cc * 128:(cc + 1) * 128], uT[kc][c4].ap[:, cc * 128:(cc + 1) * 128], v[:, kc, 0:128],
                           kc == 0, kc == 7, [slot, uT[kc][c4]], [pv])
                pvv = pv.ap.rearrange("p (c g d) -> p c g d", c=4, g=2)
                for g in range(2):
                    dcopy(gv[c4].ap[:, :, 1 + 2 * g, :], pvv[:, :, g, :], [pv], [gv[c4]])
            if stop_stage == 31:
                return
            nb_g = nbias.ap[:, l * 2 + 1:l * 2 + 2]
            for p in range(4):
                g = p // 2
                for half in range(2):
                    prow = slice(half * 64, half * 64 + 64)
                    voff = (1 + 2 * g) * 64 if half == 0 else (2 * g) * 64

                    def v_of_chunk(c, voff=voff):
                        t = gv[c // 4]
                        fl = t.ap.rearrange("p c g d -> p c (g d)")
                        return t, fl[:, c % 4, voff:voff + 128]
                    attention(gkz[g][half], gq[p], slice(0, 128), 64, v_of_chunk, 64.0 ** -0.5, nb_g, og[p], half)
            if stop_stage == 32:
                return
            dma_sp(tabC.ap, tabs_d[2], [], [tabC], "tabC")
            dma_sp(tabS.ap, tabs_d[3], [], [tabS], "tabS")
            slA, vA = wload([(wi[:, 0:256], 256)], 8)
            slB, vB = wload([(wi[:, 256:512], 256)], 8)
            slC, vC = wload([(wi[:, 512:640], 128)], 8)
            srcs = [(slA, vA, 0), (slA, vA, 128), (slB, vB, 0), (slB, vB, 128), (slC, vC, 0)]
            for tt in range(NT):
                pzq = []
                for m in range(3):
                    sl, vv, c0 = srcs[m]
                    pz = bank("Zq", [0, 1, 2, 3, 4, 5])
                    pzq.append(pz)
                    for kc in range(8):
                        mm(pz.ap, vv[:, kc, c0:c0 + 128], uT[kc][tt].ap, kc == 0, kc == 7, [sl, uT[kc][tt]], [pz])
                sb = bank("stat", [6, 7])
                for m in range(3):
                    sq = ring("sq", sq_r)
                    act(sq.ap, pzq[m].ap, AF.Square, [pzq[m]], [sq])
                    mm(sb.ap, ones_bf.ap[:, 0:128], sq.ap, m == 0, m == 2, [ones_bf, sq], [sb])
                rs = rstd_from(sb.ap, 128, 384.0, [sb])
                for m in range(3):
                    stt(qn[m][tt].ap, pzq[m].ap, gc(GC_QA + l * 3 + m), rs.ap, ALU.mult, ALU.mult, [pzq[m], gcols, rs], [qn[m][tt]])
                pzk = []
                for m in range(2):
                    sl, vv, c0 = srcs[3 + m]
                    pz = bank("Zq", [0, 1, 2, 3, 4, 5])
                    pzk.append(pz)
                    for kc in range(8):
                        mm(pz.ap, vv[:, kc, c0:c0 + 128], uT[kc][tt].ap, kc == 0, kc == 7, [sl, uT[kc][tt]], [pz])
                sb = bank("stat", [6, 7])
                for m in range(2):
                    sq = ring("sq", sq_r)
                    act(sq.ap, pzk[m].ap, AF.Square, [pzk[m]], [sq])
                    mm(sb.ap, ones_bf.ap[:, 0:128], sq.ap, m == 0, m == 1, [ones_bf, sq], [sb])
                rs = rstd_from(sb.ap, 128, 256.0, [sb])
                for m in range(2):
                    stt(kvn[m][tt].ap, pzk[m].ap, gc(GC_KVA + l * 2 + m), rs.ap, ALU.mult, ALU.mult, [pzk[m], gcols, rs], [kvn[m][tt]])
            slot, v = wload([(w_krpad_d[l], 96), (w_krswpad_d[l], 96)], 8)
            kr = slice(64, 96)
            for tt in range(NT):
                pz = bank("Zq", [0, 1, 2, 3, 4, 5])
                pzs = bank("Zq", [0, 1, 2, 3, 4, 5])
                for kc in range(8):
                    mm(pz.ap[0:96], v[:, kc, 0:96], uT[kc][tt].ap, kc == 0, kc == 7, [slot, uT[kc][tt]], [pz])
                for kc in range(8):
                    mm(pzs.ap[0:96], v[:, kc, 96:192], uT[kc][tt].ap, kc == 0, kc == 7, [slot, uT[kc][tt]], [pzs])
                act(sqk[tt].ap, pz.ap[kr], AF.Square, [pz], [sqk[tt]])
                ta = ring("tA", tA_r)
                tb = ring("tB", tB_r)
                cs = slice(tt * TT, (tt + 1) * TT)
                stt(ta.ap[kr], pz.ap[kr], gc(gx + 2)[kr], tabC.ap[kr, cs], ALU.mult, ALU.mult, [pz, gcols, tabC], [ta])
                stt(tb.ap[kr], pzs.ap[kr], gc(gx + 3)[kr], tabS.ap[kr, cs], ALU.mult, ALU.mult, [pzs, gcols, tabS], [tb])
                tt_(gbk[tt].ap, ta.ap[kr], tb.ap[kr], ALU.add, [ta, tb], [gbk[tt]])
            if stop_stage == 33:
                return
            nb_m = nbias.ap[:, l * 2:l * 2 + 1]
            r96 = slice(0, 96)
            r64 = slice(0, 64)
            for tt in range(NT):
                P.add("dve", (lambda a=kThF[tt].ap[96:128]: lambda e: e.memset(a, 0.0))(), [], [kThF[tt]])
                P.add("dve", (lambda a=qThF[tt].ap[96:128]: lambda e: e.memset(a, 0.0))(), [], [qThF[tt]])
            for h in range(8):
                p, half = h // 2, h % 2
                s = wctr[0] % NWS
                wctr[0] += 1
                slot = wslot[s]
                vq = slot.ap[:, 0:288].rearrange("p (k c) -> p k c", k=3)
                vqs = slot.ap[:, 288:576].rearrange("p (k c) -> p k c", k=3)
                vkv = slot.ap[:, 576:832].rearrange("p (k c) -> p k c", k=2)
                dma_pool(vq, w_uq_d[l][:, h * 96:(h + 1) * 96].rearrange("(k p) c -> p k c", p=128), [], [slot], ("w", s))
                dma_pool(vqs, w_uqsw_d[l][:, h * 96:(h + 1) * 96].rearrange("(k p) c -> p k c", p=128), [], [slot], ("w", s))
                dma_pool(vkv, w_ukv_d[l][:, h * 128:(h + 1) * 128].rearrange("(k p) c -> p k c", p=128), [], [slot], ("w", s))
                pks = []
                for tt in range(NT):
                    pk = PS[tt]
                    pks.append(pk)
                    for kc in range(2):
                        mm(pk.ap[r64], vkv[:, kc, 0:64], kvn[kc][tt].ap, kc == 0, kc == 1, [slot, kvn[kc][tt]], [pk])
                for blkk in (0, 2):
                    P.add("dve", (lambda a=Vh.ap[:, :, blkk, :]: lambda e: e.memset(a, 1.0))(), [], [Vh])
                for c8 in range(2):
                    pv = PS[4 + c8]
                    for cc in range(8):
                        c = c8 * 8 + cc
                        for kc in range(2):
                            mm(pv.ap[:, cc * 64:(cc + 1) * 64], kvn[kc][c // 4].ap[:, (c % 4) * 128:(c % 4 + 1) * 128],
                               vkv[:, kc, 64:128], kc == 0, kc == 1, [slot, kvn[kc][c // 4]], [pv])
                    dcopy(Vh.ap[:, c8 * 8:(c8 + 1) * 8, 1, :], pv.ap.rearrange("p (c d) -> p c d", c=8), [pv], [Vh])
                for tt in range(NT):
                    pk = pks[tt]
                    P.add("dve", (lambda o=gbn[tt].ap, i_=pk.ap[r64], sc=gc(gx + 2)[r64]:
                                  lambda e: e.tensor_scalar(o, i_, sc, None, ALU.mult))(), [pk, gcols], [gbn[tt]])
                    act(sqn[tt].ap, pk.ap[r64], AF.Square, [pk], [sqn[tt]])
                for tt in range(NT):
                    sb = bank("stat", [6, 7])
                    sqfull = view(OFF_R1 + 28672 + tt * TT * 2, TT * 2, BF16)[r96]
                    mm(sb.ap[r96], ones_bf.ap[r96, 0:96], sqfull, True, True, [ones_bf, sqn[tt], sqk[tt]], [sb])
                    rs = rstd_from(sb.ap[r96], 96, 96.0, [sb])
                    gfull = view(OFF_R1 + 20480 + tt * TT * 4, TT * 4, F32)[r96]
                    tt_(kTh[tt].ap, gfull, rs.ap[r96], ALU.mult, [gbn[tt], gbk[tt], rs], [kTh[tt]])
                pqs_l = {}

                def q_mms(tt):
                    pq = PS[tt]
                    pqs = PS[4 + tt % 2]
                    pqs_l[tt] = (pq, pqs)
                    for kc in range(3):
                        mm(pq.ap[r96], vq[:, kc, :], qn[kc][tt].ap, kc == 0, kc == 2, [slot, qn[kc][tt]], [pq])
                    for kc in range(3):
                        mm(pqs.ap[r96], vqs[:, kc, :], qn[kc][tt].ap, kc == 0, kc == 2, [slot, qn[kc][tt]], [pqs])

                q_mms(0)
                q_mms(1)
                for tt in range(NT):
                    pq, pqs = pqs_l[tt]
                    finalize_rope(pq, pqs, 96, ones_bf.ap[r96, 0:96], 96.0, gx + 0, gx + 1, tt, qTh[tt])
                    if tt + 2 < NT:
                        q_mms(tt + 2)
                voff = 64 if half == 0 else 0

                def v_of_chunk(c, voff=voff):
                    fl = Vh.ap.rearrange("p c g d -> p c (g d)")
                    return Vh, fl[:, c, voff:voff + 128]
                attention(kThF, qThF, slice(0, 128), 96, v_of_chunk, 96.0 ** -0.5, nb_m, om[p], half)
            if stop_stage == 34:
                return
            for hf in range(2):
                tts = [hf * 2, hf * 2 + 1]
                norm_mod(l, b, 1, tts, upT)
                for dch in range(DC):
                    slG, vG = wload([(wi[:, 1440 + dch * 128:1440 + (dch + 1) * 128], 128),
                                     (wi[:, 2464 + dch * 128:2464 + (dch + 1) * 128], 128)], 8)
                    slB_, vBr = wload([(w_bm_d[l][:, dch * 128:(dch + 1) * 128], 128),
                                       (w_bg_d[l][:, dch * 128:(dch + 1) * 128], 128)], 4)
                    for t2 in range(2):
                        tt = tts[t2]
                        pg1 = bank("A", [0, 1])
                        pg2 = bank("B", [2, 3])
                        pb1 = bank("O", [4, 5])
                        pb2 = bank("stat", [6, 7])
                        for kc in range(8):
                            mm(pg1.ap, vG[:, kc, 0:128], upT[kc][t2].ap, kc == 0, kc == 7, [slG, upT[kc][t2]], [pg1])
                        for kc in range(8):
                            mm(pg2.ap, vG[:, kc, 128:256], upT[kc][t2].ap, kc == 0, kc == 7, [slG, upT[kc][t2]], [pg2])
                        for p in range(4):
                            ofull = view(OFF_UO + (p * T + tt * TT) * 2, TT * 2, BF16)
                            mm(pb1.ap, vBr[:, p, 0:128], ofull, p == 0, p == 3, [slB_, om[p][tt][0], om[p][tt][1]], [pb1])
                        for p in range(4):
                            ofull = view(OFF_OG + (p * T + tt * TT) * 2, TT * 2, BF16)
                            mm(pb2.ap, vBr[:, p, 128:256], ofull, p == 0, p == 3, [slB_, og[p][tt][0], og[p][tt][1]], [pb2])
                        ta = ring("tA", tA_r)
                        tb = ring("tB", tB_r)
                        act(ta.ap, pg1.ap, AF.Sigmoid, [pg1], [ta])
                        act(tb.ap, pg2.ap, AF.Sigmoid, [pg2], [tb])
                        tt_(ta.ap, ta.ap, pb1.ap, ALU.mult, [ta, pb1], [ta])
                        tt_(tb.ap, tb.ap, pb2.ap, ALU.mult, [tb, pb2], [tb])
                        tt_(mT[dch][t2].ap, ta.ap, tb.ap, ALU.add, [ta, tb], [mT[dch][t2]])
                for dc in range(DC):
                    slO, vO = wload([(w_out_d[l][:, dc * 128:(dc + 1) * 128], 128)], 8)
                    for t2 in range(2):
                        tt = tts[t2]
                        po = bank("O", [4, 5])
                        for k in range(8):
                            mm(po.ap, vO[:, k, :], mT[k][t2].ap, k == 0, k == 7, [slO, mT[k][t2]], [po])
                        stt(hT[dc][tt].ap, po.ap, modcol(l, 5, dc, b), hT[dc][tt].ap, ALU.mult, ALU.add,
                            [po, modT, hT[dc][tt]], [hT[dc][tt]])

        for b in range(nseq):
            for ti in range(16):
                st = xs[ti % 2]
                dma_sp(st.ap, x_d[b, ti * 128:(ti + 1) * 128, :], [], [st], ("xs", ti % 2))
                tt, k = ti // 4, ti % 4
                for hfb in range(2):
                    pb_ = bank("tr", [6, 7])
                    for d4 in range(4):
                        dc = hfb * 4 + d4
                        P.add("pe", (lambda o=pb_.ap[:, d4 * 128:(d4 + 1) * 128], i_=st.ap[:, dc * 128:(dc + 1) * 128]:
                                     lambda e: e.transpose(o, i_, ident.ap[:, 0:128]))(), [st, ident], [pb_])
                    hv = view(OFF_H, 65536, F32).rearrange("p (d t) -> p d t", d=DC)
                    dcopy(hv[:, hfb * 4:hfb * 4 + 4, tt * TT + k * 128:tt * TT + (k + 1) * 128],
                          pb_.ap.rearrange("p (d t) -> p d t", d=4), [pb_], [hT[hfb * 4 + d4][tt] for d4 in range(4)])
            for l in range(depth):
                for i in range(3):
                    stt(gsvv[:, i, :], modv[:, l, (3 * i + 1) * 8:(3 * i + 2) * 8, b], 1.0, n3v[:, l, i, :], ALU.add, ALU.mult,
                        [modT, norm3], [gsv])
                for hi_, i in ((0, 0), (1, 2)):
                    P.add("dve", (lambda o=hgvv[:, hi_, :], a=modv[:, l, (3 * i + 2) * 8:(3 * i + 3) * 8, b]:
                                  lambda e: e.tensor_scalar(o, a, 0.5, None, ALU.mult))(), [modT], [hgv])
                if stop_stage == 1:
                    break
                ffn(l, b, 0)
                if stop_stage == 2:
                    break
                mixer(l, b)
                if stop_stage == 3 or stop_stage > 30:
                    break
                ffn(l, b, 1)
            hv = view(OFF_H, 65536, F32).rearrange("p (d t) -> p d t", d=DC)
            for ti in range(16):
                st = ys[ti % 2]
                tt, k = ti // 4, ti % 4
                for hfb in range(2):
                    pb_ = bank("tr", [6, 7])
                    for d4 in range(4):
                        dc = hfb * 4 + d4
                        P.add("pe", (lambda o=pb_.ap[:, d4 * 128:(d4 + 1) * 128],
                                     i_=hT[dc][tt].ap[:, k * 128:(k + 1) * 128]:
                                     lambda e: e.transpose(o, i_, ident.ap[:, 0:128]))(), [hT[dc][tt], ident], [pb_])
                    dcopy(st.ap[:, hfb * 512:(hfb + 1) * 512], pb_.ap, [pb_], [st])
                dma_sp(out_d[b, ti * 128:(ti + 1) * 128, :], st.ap, [st], [], ("ys", ti % 2))
        P.add("sp", lambda e: e.nop(), [], [ys[0], ys[1]])
        P.emit(nc, es)
    return nc


def _rope_tables():
    f32 = np.float32

    def tables(pos, dim):
        inv = (np.float32(10000.0) ** (-(np.arange(0, dim, 2, dtype=f32)) / f32(dim))).astype(f32)
        ang = pos.astype(f32)[:, None] * inv[None, :]
        return np.cos(ang).astype(f32), np.sin(ang).astype(f32)

    t = np.arange(T)
    row = t // 64
    col = t % 64
    pc, ps = tables(t, 32)
    rc, rs = tables(row, 32)
    cc, cs = tables(col, 32)
    tabs = np.zeros((4, 128, T), f32)
    Cg = np.concatenate([rc.T, rc.T, cc.T, cc.T], axis=0)
    Sg = np.concatenate([-rs.T, rs.T, -cs.T, cs.T], axis=0)
    tabs[0] = np.concatenate([Cg, Cg], axis=0)
    tabs[1] = np.concatenate([Sg, Sg], axis=0)
    tabs[2, 0:64] = 1.0
    tabs[2, 64:96] = np.concatenate([pc.T, pc.T], axis=0)
    tabs[3, 64:96] = np.concatenate([-ps.T, ps.T], axis=0)
    return tabs


def _swap_idx(n, half):
    idx = np.arange(n)
    blk = idx // (2 * half)
    r = idx % (2 * half)
    return blk * 2 * half + (r + half) % (2 * half)


def _prep_shared(inp):
    f32 = np.float32
    A = lambda k: np.ascontiguousarray(np.asarray(inp[k], dtype=f32))
    w_in = A("w_in")
    sh = {}
    sh["w_ada"] = A("w_ada")
    sh["b_adaT"] = np.ascontiguousarray(A("b_ada").reshape(L, 72, 128).transpose(0, 2, 1))
    n3 = np.stack([A("norm_ffn1"), A("norm_mix"), A("norm_ffn2")], axis=1)
    sh["norm3"] = np.ascontiguousarray(n3.reshape(L, 3, 8, 128).transpose(3, 0, 1, 2).reshape(128, L * 24))
    for k in ("w_ffn1_in", "w_ffn1_out", "w_ffn2_in", "w_ffn2_out", "mla_w_uq", "mla_w_ukv", "w_branch_mla",
              "w_branch_gqa", "w_out"):
        sh[k] = A(k)
    sh["w_in"] = w_in
    sw16 = _swap_idx(512, 16)
    gq = w_in[:, :, 672:1184]
    sh["w_gq_sw"] = np.ascontiguousarray(gq[:, :, sw16])
    gkc = w_in[:, :, 1184:1312]
    gks = gkc[:, :, _swap_idx(128, 16)]
    rep = lambda a: np.ascontiguousarray(np.concatenate([a[:, :, 0:64], a[:, :, 0:64], a[:, :, 64:128], a[:, :, 64:128]], axis=2))
    sh["w_gk_rep"] = rep(gkc)
    sh["w_gk_sw_rep"] = rep(gks)
    kr = w_in[:, :, 640:672]
    z64 = np.zeros((L, D, 64), f32)
    sh["w_kr_pad"] = np.ascontiguousarray(np.concatenate([z64, kr], axis=2))
    sh["w_kr_sw_pad"] = np.ascontiguousarray(np.concatenate([z64, kr[:, :, _swap_idx(32, 16)]], axis=2))
    uq = A("mla_w_uq").reshape(L, 384, 8, 96)
    uqsw = np.zeros_like(uq)
    uqsw[:, :, :, 64:96] = uq[:, :, :, 64:96][:, :, :, _swap_idx(32, 16)]
    sh["w_uq_sw"] = np.ascontiguousarray(uqsw.reshape(L, 384, 768))
    NG = L * 3 + L * 2 + 8 * L
    gcols = np.zeros((128, NG), f32)
    qa = A("mla_q_a_norm")
    kva = A("mla_kv_a_norm")
    qkq = A("mla_qk_q_norm")
    qkk = A("mla_qk_k_norm")
    gqn = A("gqa_q_norm")
    gkn = A("gqa_k_norm")
    s32 = _swap_idx(32, 16)
    s64 = _swap_idx(64, 16)
    for l in range(L):
        gcols[:, l * 3:(l + 1) * 3] = qa[l].reshape(3, 128).T
        gcols[:, L * 3 + l * 2:L * 3 + (l + 1) * 2] = kva[l].reshape(2, 128).T
        gx = L * 3 + L * 2 + 8 * l
        gcols[0:96, gx + 0] = qkq[l]
        gcols[64:96, gx + 1] = qkq[l][64:96][s32]
        gcols[0:96, gx + 2] = qkk[l]
        gcols[64:96, gx + 3] = qkk[l][64:96][s32]
        gcols[:, gx + 4] = np.concatenate([gqn[l], gqn[l]])
        gcols[:, gx + 5] = np.concatenate([gqn[l][s64], gqn[l][s64]])
        gcols[:, gx + 6] = np.concatenate([gkn[l], gkn[l]])
        gcols[:, gx + 7] = np.concatenate([gkn[l][s64], gkn[l][s64]])
    sh["gcols"] = gcols
    grow = np.zeros((1, L * 4 * 96), f32)
    for l in range(L):
        grow[0, (l * 4 + 0) * 96:(l * 4 + 0) * 96 + 96] = qkq[l]
        grow[0, (l * 4 + 1) * 96:(l * 4 + 1) * 96 + 96] = qkk[l]
        grow[0, (l * 4 + 2) * 96:(l * 4 + 2) * 96 + 64] = gqn[l]
        grow[0, (l * 4 + 3) * 96:(l * 4 + 3) * 96 + 64] = gkn[l]
    sh["grow"] = grow
    sh["tabs"] = _rope_tables()
    sh["ident"] = np.eye(128, dtype=f32)
    return sh


_PROG_CACHE = {}


def _get_prog(nseq, depth=L, stop_stage=0):
    key = (nseq, depth, stop_stage)
    if key not in _PROG_CACHE:
        _PROG_CACHE[key] = build_program(nseq, depth, stop_stage)
    return _PROG_CACHE[key]


def kernel(**inputs):
    x = np.asarray(inputs["x"], dtype=np.float32)
    c = np.asarray(inputs["c"], dtype=np.float32)
    B = x.shape[0]
    nseq = B // NCORES
    sh = _prep_shared(inputs)
    in_maps = []
    for i in range(NCORES):
        m = dict(sh)
        m["x"] = np.ascontiguousarray(x[i * nseq:(i + 1) * nseq])
        m["cT"] = np.ascontiguousarray(c[i * nseq:(i + 1) * nseq].T)
        in_maps.append(m)
    nc = _get_prog(nseq)
    res = run_bass_kernel_spmd(nc, in_maps, core_ids=list(range(NCORES)))
    return np.concatenate([np.asarray(r["out"]) for r in res.results], axis=0).astype(np.float32)
```

```python
import numpy as np
from contextlib import ExitStack
import concourse.bass as bass
import concourse.mybir as mybir
from concourse.bass_utils import run_bass_kernel_spmd

F32 = mybir.dt.float32
BF16 = mybir.dt.bfloat16
AF = mybir.ActivationFunctionType
ALU = mybir.AluOpType
AX = mybir.AxisListType

NCORES = 8
L = 2
D = 1024
DC = 8
T = 2048
TT = 512
NT = 4
FF = 2816
FC = 22
DIN = 3488
EPS = 1e-6
SAME_ENG_RAW_SYNC = True


class Tile:
    __slots__ = ("name", "ap", "lo", "hi", "plo", "phi", "lw", "rd", "ov", "ps")

    def __init__(self, name, ap, lo, hi, plo, phi):
        self.name, self.ap, self.lo, self.hi, self.plo, self.phi = name, ap, lo, hi, plo, phi
        self.ps = False
        self.lw = None
        self.rd = {}
        self.ov = []


class Op:
    __slots__ = ("idx", "eng", "fn", "deps", "dma_key", "dma_n", "sig_needed", "sig", "group")

    def __init__(self, idx, eng, fn, deps, dma_key, group):
        self.idx, self.eng, self.fn, self.deps, self.dma_key, self.group = idx, eng, fn, deps, dma_key, group
        self.dma_n = 0
        self.sig_needed = False
        self.sig = 0


class Prog:
    def __init__(self):
        self.ops = []
        self.tiles = []
        self.dma_cnt = {}
        self.space_tiles = {}

    def tile(self, name, ap, space, lo, hi, plo=0, phi=128):
        t = Tile(name, ap, lo, hi, plo, phi)
        t.ps = (space == "ps")
        lst = self.space_tiles.setdefault(space, [])
        for o in lst:
            if o.lo < hi and lo < o.hi and o.plo < phi and plo < o.phi:
                t.ov.append(o)
                o.ov.append(t)
        lst.append(t)
        return t

    def _need(self, a, eng, is_dma, kind):
        if a.dma_key is not None:
            return True
        if (not is_dma) and a.eng == eng:
            return SAME_ENG_RAW_SYNC and eng != "pe"
        return True

    def add(self, eng, fn, reads=(), writes=(), dma_key=None, group=False):
        idx = len(self.ops)
        deps = {}
        for t in reads:
            if t.lw is not None:
                deps[t.lw] = 0
            for x in t.ov:
                if x.lw is not None:
                    deps[x.lw] = 0
            if t.ps:
                for rk_, r in t.rd.items():
                    if rk_ != eng and r not in deps:
                        deps[r] = 2
        for t in writes:
            for x in [t] + t.ov:
                if x.lw is not None and x.lw not in deps:
                    deps[x.lw] = 1
                for r in x.rd.values():
                    if r not in deps:
                        deps[r] = 2
        is_dma = dma_key is not None
        fdeps = []
        latest = {}
        for a_idx, kind in deps.items():
            a = self.ops[a_idx]
            if self._need(a, eng, is_dma, kind):
                if a.dma_key is None:
                    if a.eng not in latest or latest[a.eng] < a_idx:
                        latest[a.eng] = a_idx
                else:
                    fdeps.append(a_idx)
        for a_idx in latest.values():
            fdeps.append(a_idx)
            self.ops[a_idx].sig_needed = True
        op = Op(idx, eng, fn, fdeps, dma_key, group)
        if is_dma:
            self.dma_cnt[dma_key] = self.dma_cnt.get(dma_key, 0) + 1
            op.dma_n = self.dma_cnt[dma_key]
        rk = ("d", dma_key) if is_dma else eng
        for t in reads:
            t.rd[rk] = idx
        for t in writes:
            t.lw = idx
            t.rd = {}
        self.ops.append(op)
        return op

    def emit(self, nc, es):
        engs = ["pe", "act", "dve", "pool", "sp"]
        cnt = {e: 0 for e in engs}
        for op in self.ops:
            if op.dma_key is None and op.sig_needed:
                cnt[op.eng] += 1
                op.sig = cnt[op.eng]
        esem = {e: es.enter_context(nc.semaphore("sem_" + e)) for e in engs}
        dsem = {k: es.enter_context(nc.semaphore("dsem_%d" % i)) for i, k in enumerate(sorted(self.dma_cnt, key=str))}
        block = es.enter_context(nc.Block())
        ops = self.ops
        dma_cnt = self.dma_cnt

        def stream(ename):
            def body(eng):
                waited = {}
                for op in ops:
                    if op.eng != ename:
                        continue
                    need = {}
                    for a_idx in op.deps:
                        a = ops[a_idx]
                        if a.dma_key is not None:
                            sem = dsem[a.dma_key]
                            val = 16 * (dma_cnt[a.dma_key] if a.group else a.dma_n)
                        else:
                            sem = esem[a.eng]
                            val = a.sig
                        k = id(sem)
                        if k not in need or need[k][1] < val:
                            need[k] = (sem, val)
                    for k, (sem, val) in need.items():
                        if waited.get(k, 0) < val:
                            eng.wait_ge(sem, val)
                            waited[k] = val
                    inst = op.fn(eng)
                    if op.dma_key is not None:
                        inst.then_inc(dsem[op.dma_key], 16)
                    elif op.sig_needed:
                        inst.then_inc(esem[ename], 1)
            return body

        block.tensor(stream("pe"))
        block.scalar(stream("act"))
        block.vector(stream("dve"))
        block.gpsimd(stream("pool"))
        block.sync(stream("sp"))


def build_program(nseq, depth=L, stop_stage=0):
    nc = bass.Bass("TRN2", target_bir_lowering=False)
    P = Prog()

    def din(name, shape):
        return nc.dram_tensor(name, list(shape), F32, kind="ExternalInput").ap()

    x_d = din("x", [nseq, T, D])
    cT_d = din("cT", [D, nseq])
    w_ada_d = din("w_ada", [L, D, 9 * D])
    b_adaT_d = din("b_adaT", [L, 128, 72])
    norm3_d = din("norm3", [128, L * 3 * 8])
    w_f1i_d = din("w_ffn1_in", [L, D, 2 * FF])
    w_f1o_d = din("w_ffn1_out", [L, FF, D])
    w_f2i_d = din("w_ffn2_in", [L, D, 2 * FF])
    w_f2o_d = din("w_ffn2_out", [L, FF, D])
    w_in_d = din("w_in", [L, D, DIN])
    w_gqsw_d = din("w_gq_sw", [L, D, 512])
    w_gkrep_d = din("w_gk_rep", [L, D, 256])
    w_gkswrep_d = din("w_gk_sw_rep", [L, D, 256])
    w_krpad_d = din("w_kr_pad", [L, D, 96])
    w_krswpad_d = din("w_kr_sw_pad", [L, D, 96])
    w_uq_d = din("mla_w_uq", [L, 384, 768])
    w_uqsw_d = din("w_uq_sw", [L, 384, 768])
    w_ukv_d = din("mla_w_ukv", [L, 256, 1024])
    w_bm_d = din("w_branch_mla", [L, 512, D])
    w_bg_d = din("w_branch_gqa", [L, 512, D])
    w_out_d = din("w_out", [L, D, D])
    NG = L * 3 + L * 2 + 8 * L
    gcols_d = din("gcols", [128, NG])
    grow_d = din("grow", [1, L * 4 * 96])
    tabs_d = din("tabs", [4, 128, T])
    ident_d = din("ident", [128, 128])
    out_d = nc.dram_tensor("out", [nseq, T, D], F32, kind="ExternalOutput").ap()

    es = ExitStack()
    with es:
        OFF_H = 0
        OFF_UO = OFF_H + 65536
        OFF_OG = OFF_UO + 32768
        OFF_R1 = OFF_OG + 16384
        OFF_T = OFF_R1 + 34816
        OFF_W = OFF_T + 16384
        NWS = 3
        OFF_S = OFF_W + NWS * 4096
        OFF_P = OFF_S + 16384
        OFF_C = OFF_P + 4096
        ARENA = OFF_C + 8192
        arena = es.enter_context(nc.sbuf_tensor("arena", [128, ARENA // 2], BF16))
        psall = es.enter_context(nc.psum_tensor("psall", [128, 4096], F32))

        def view(off, nbytes, dt):
            a = arena[:, off // 2:(off + nbytes) // 2]
            if dt == F32:
                a = a.bitcast(F32)
            return a

        def mk(name, off, nbytes, dt, plo=0, phi=128, shape=None):
            a = view(off, nbytes, dt)
            if shape is not None:
                if len(shape) == 1:
                    a = a.rearrange("p (a b) -> p a b", a=shape[0])
                elif len(shape) == 2:
                    a = a.rearrange("p (a b c) -> p a b c", a=shape[0], b=shape[1])
            if plo != 0 or phi != 128:
                a = a[plo:phi]
            return P.tile(name, a, "sb", off, off + nbytes, plo, phi)

        PS = [P.tile("ps%d" % i, psall[:, i * 512:(i + 1) * 512], "ps", i * 2048, (i + 1) * 2048) for i in range(8)]

        hT = [[mk("h", OFF_H + (dc * T + tt * TT) * 4, TT * 4, F32) for tt in range(NT)] for dc in range(DC)]
        uT = [[mk("u", OFF_UO + (dc * T + tt * TT) * 2, TT * 2, BF16) for tt in range(NT)] for dc in range(DC)]
        om = [[[mk("om", OFF_UO + (p * T + tt * TT) * 2, TT * 2, BF16, h * 64, h * 64 + 64) for h in range(2)]
               for tt in range(NT)] for p in range(4)]
        og = [[[mk("og", OFF_OG + (p * T + tt * TT) * 2, TT * 2, BF16, h * 64, h * 64 + 64) for h in range(2)]
               for tt in range(NT)] for p in range(4)]
        OFF_PH = OFF_UO + 16384
        kTh = [mk("kTh", OFF_PH + tt * TT * 2, TT * 2, BF16, 0, 96) for tt in range(NT)]
        qTh = [mk("qTh", OFF_PH + 4096 + tt * TT * 2, TT * 2, BF16, 0, 96) for tt in range(NT)]
        kThF = [mk("kThF", OFF_PH + tt * TT * 2, TT * 2, BF16) for tt in range(NT)]
        qThF = [mk("qThF", OFF_PH + 4096 + tt * TT * 2, TT * 2, BF16) for tt in range(NT)]
        Vh = mk("Vh", OFF_PH + 8192, 16 * 192 * 2, BF16, shape=(16, 3))
        gq = [[mk("gq", OFF_R1 + (p * T + tt * TT) * 2, TT * 2, BF16) for tt in range(NT)] for p in range(4)]
        gkz = [[[mk("gkz", OFF_R1 + 16384 + ((g * 2 + hf) * T + tt * TT) * 2, TT * 2, BF16) for tt in range(NT)]
                for hf in range(2)] for g in range(2)]
        gv = [mk("gv", OFF_T + c4 * 2560, 2560, BF16, shape=(4, 5)) for c4 in range(4)]
        qn = [[mk("qn", OFF_R1 + (m * T + tt * TT) * 2, TT * 2, BF16) for tt in range(NT)] for m in range(3)]
        kvn = [[mk("kvn", OFF_R1 + 12288 + (m * T + tt * TT) * 2, TT * 2, BF16) for tt in range(NT)] for m in range(2)]
        gbn = [mk("gbn", OFF_R1 + 20480 + tt * TT * 4, TT * 4, F32, 0, 64) for tt in range(NT)]
        gbk = [mk("gbk", OFF_R1 + 20480 + tt * TT * 4, TT * 4, F32, 64, 96) for tt in range(NT)]
        sqn = [mk("sqn", OFF_R1 + 28672 + tt * TT * 2, TT * 2, BF16, 0, 64) for tt in range(NT)]
        sqk = [mk("sqk", OFF_R1 + 28672 + tt * TT * 2, TT * 2, BF16, 64, 96) for tt in range(NT)]
        yT = [[mk("y", OFF_R1 + (j * T + tt * TT) * 2, TT * 2, BF16) for tt in range(NT)] for j in range(6)]
        upT = [[mk("up", OFF_R1 + (dc * 1024 + t2 * TT) * 2, TT * 2, BF16) for t2 in range(2)] for dc in range(DC)]
        mT = [[mk("mT", OFF_R1 + 16384 + (dc * 1024 + t2 * TT) * 2, TT * 2, BF16) for t2 in range(2)] for dc in range(DC)]
        tabC = mk("tabC", OFF_T, 8192, F32)
        tabS = mk("tabS", OFF_T + 8192, 8192, F32)
        wout = [mk("wout", OFF_T + s * 2048, 2048, BF16) for s in range(6)]
        xs = [mk("xs", OFF_T + s * 4096, 4096, F32) for s in range(2)]
        ys = [mk("ys", OFF_T + 8192 + s * 4096, 4096, F32) for s in range(2)]
        wslot = [mk("w", OFF_W + s * 4096, 4096, BF16) for s in range(NWS)]
        rstd_r = [mk("rstd", OFF_S + s * 2048, 2048, F32) for s in range(2)]
        tA_r = [mk("tA", OFF_S + 4096 + s * 2048, 2048, F32) for s in range(2)]
        tB_r = [mk("tB", OFF_S + 8192 + s * 2048, 2048, F32) for s in range(2)]
        sq_r = [mk("sq", OFF_S + 12288 + s * 1024, 1024, BF16) for s in range(2)]
        rinv = mk("rinv", OFF_S + 14336, 2048, F32)
        PT = [mk("pt", OFF_P + s * 2048, 2048, BF16) for s in range(2)]
        co = [OFF_C]

        def cmk(name, nbytes, dt, shape=None):
            nb = (nbytes + 31) // 32 * 32
            t = mk(name, co[0], nb, dt, shape=None)
            co[0] += nb
            assert co[0] <= ARENA
            return t

        ident = cmk("ident", 512, F32)
        ones_bf = cmk("ones", 256, BF16)
        blk_bf = cmk("blk", 256, BF16)
        ones_f = cmk("ones_f", 512, F32)
        modT = cmk("modT", L * 72 * 4 * 4, F32)
        badaT = cmk("badaT", L * 72 * 4, F32)
        norm3 = cmk("norm3", L * 3 * 8 * 4, F32)
        gcols = cmk("gcols", NG * 4, F32)
        growt = mk("grow", OFF_T, L * 4 * 96 * 4, F32)
        gmax = cmk("gmax", L * 4 * 4, F32)
        nbias = cmk("nbias", L * 2 * 4, F32)
        epst = cmk("eps", 4, F32)
        cTt = cmk("cT", 8 * nseq * 4, F32)
        cact = cmk("cact", 8 * nseq * 2, BF16)
        gsv = cmk("gsv", 3 * 8 * 4, F32)
        hgv = cmk("hgv", 2 * 8 * 4, F32)

        rings = {}

        def ring(name, lst):
            i = rings.get(name, 0)
            rings[name] = i + 1
            return lst[i % len(lst)]

        def bank(name, ids):
            return PS[ring("bank_" + name, ids)]

        def mm(out_ap, lhsT, rhs, start, stop, reads, writes):
            P.add("pe", lambda e: e.matmul(out_ap, lhsT, rhs, start=start, stop=stop), reads, writes)

        def act(out_ap, in_ap, func, reads, writes, bias=None, scale=None):
            kw = {}
            if bias is not None:
                kw["bias"] = bias
            if scale is not None:
                kw["scale"] = scale
            P.add("act", lambda e: e.activation(out=out_ap, in_=in_ap, func=func, **kw), reads, writes)

        def stt(out_ap, in0, scalar, in1, op0, op1, reads, writes):
            P.add("dve", lambda e: e.scalar_tensor_tensor(out_ap, in0, scalar, in1, op0, op1), reads, writes)

        def tt_(out_ap, in0, in1, op, reads, writes):
            P.add("dve", lambda e: e.tensor_tensor(out_ap, in0, in1, op), reads, writes)

        def dcopy(out_ap, in_ap, reads, writes):
            P.add("dve", lambda e: e.tensor_copy(out_ap, in_ap), reads, writes)

        def dma_sp(out_ap, in_ap, reads, writes, key, group=False, slow=False):
            if slow:
                P.add("sp", lambda e: e.dma_start(out=out_ap, in_=in_ap, allow_slow_non_contiguous=True),
                      reads, writes, dma_key=key, group=group)
            else:
                P.add("sp", lambda e: e.dma_start(out=out_ap, in_=in_ap), reads, writes, dma_key=key, group=group)

        def dma_act(out_ap, in_ap, reads, writes, key):
            P.add("act", lambda e: e.dma_start(out=out_ap, in_=in_ap), reads, writes, dma_key=key)

        def dma_pool(out_ap, in_ap, reads, writes, key):
            P.add("pool", lambda e: e.dma_start(out=out_ap, in_=in_ap), reads, writes, dma_key=key)

        wctr = [0]

        def wload(pieces, kcs):
            s = wctr[0] % NWS
            wctr[0] += 1
            slot = wslot[s]
            tot = sum(n for _, n in pieces)
            assert kcs * tot <= 2048
            v = slot.ap[:, 0:kcs * tot].rearrange("p (k c) -> p k c", k=kcs)
            c0 = 0
            for src, n in pieces:
                srcv = src.rearrange("(k p) c -> p k c", p=128)
                dma_pool(v[:, :, c0:c0 + n], srcv, [], [slot], ("w", s))
                c0 += n
            return slot, v

        CK = "const"
        dma_sp(ident.ap[:, 0:128], ident_d, [], [ident], CK, group=True)
        dma_sp(badaT.ap[:, 0:L * 72].rearrange("p (l m) -> p l m", l=L), b_adaT_d.rearrange("l p m -> p l m"),
               [], [badaT], CK, group=True)
        dma_sp(norm3.ap[:, 0:L * 24], norm3_d, [], [norm3], CK, group=True)
        dma_sp(gcols.ap[:, 0:NG], gcols_d, [], [gcols], CK, group=True)
        dma_sp(growt.ap[0:1, 0:L * 4 * 96], grow_d, [], [growt], CK, group=True)
        dma_sp(cTt.ap[:, 0:8 * nseq].rearrange("p (k b) -> p k b", k=8), cT_d.rearrange("(k p) b -> p k b", p=128),
               [], [cTt], CK, group=True, slow=True)
        P.add("dve", lambda e: e.memset(ones_bf.ap[:, 0:128], 1.0), [], [ones_bf])
        P.add("dve", lambda e: e.memset(ones_f.ap[:, 0:128], 1.0), [], [ones_f])
        P.add("dve", lambda e: e.memset(blk_bf.ap[:, 0:128], 0.0), [], [blk_bf])
        P.add("dve", lambda e: e.memset(blk_bf.ap[0:64, 0:64], 1.0), [], [blk_bf])
        P.add("dve", lambda e: e.memset(blk_bf.ap[64:128, 64:128], 1.0), [], [blk_bf])
        P.add("dve", lambda e: e.memset(epst.ap[:, 0:1], EPS), [], [epst])

        gr = growt.ap[0:1, 0:L * 4 * 96].rearrange("p (a b) -> p a b", b=96)
        P.add("dve", lambda e: e.tensor_reduce(gmax.ap[0:1, 0:L * 4], gr, AX.X, ALU.max, apply_absolute_value=True),
              [growt], [gmax])
        for l in range(L):
            for mi, dd in ((0, 96.0), (1, 64.0)):
                a = gmax.ap[0:1, l * 4 + 2 * mi:l * 4 + 2 * mi + 1]
                b = gmax.ap[0:1, l * 4 + 2 * mi + 1:l * 4 + 2 * mi + 2]
                P.add("dve", (lambda a=a, b=b, dd=dd: lambda e: e.scalar_tensor_tensor(
                    a, a, -float(np.sqrt(dd)), b, ALU.mult, ALU.mult))(), [gmax], [gmax])
        bk = PS[7]
        mm(bk.ap[:, 0:L * 4], ones_f.ap[0:1, 0:128], gmax.ap[0:1, 0:L * 4], True, True, [ones_f, gmax], [bk])
        for l in range(L):
            for mi in range(2):
                src = bk.ap[:, l * 4 + 2 * mi:l * 4 + 2 * mi + 1]
                dst = nbias.ap[:, l * 2 + mi:l * 2 + mi + 1]
                dcopy(dst, src, [bk], [nbias])

        act(cact.ap[:, 0:8 * nseq], cTt.ap[:, 0:8 * nseq], AF.Silu, [cTt], [cact])
        cactv = cact.ap[:, 0:8 * nseq].rearrange("p (k b) -> p k b", k=8)
        modv = modT.ap[:, 0:L * 72 * 4].rearrange("p (l m b) -> p l m b", l=L, m=72)
        badv = badaT.ap[:, 0:L * 72].rearrange("p (l m) -> p l m", l=L)
        for l in range(depth):
            mb = PS[6]
            for s in range(36):
                slot, v = wload([(w_ada_d[l][:, s * 256:(s + 1) * 256], 256)], 8)
                for mloc in range(2):
                    m = s * 2 + mloc
                    for kc in range(8):
                        mm(mb.ap[:, m * nseq:(m + 1) * nseq], v[:, kc, mloc * 128:(mloc + 1) * 128], cactv[:, kc, :],
                           kc == 0, kc == 7, [slot, cact], [mb])
            mbv = mb.ap[:, 0:72 * nseq].rearrange("p (m b) -> p m b", b=nseq)
            for b in range(nseq):
                tt_(modv[:, l, :, b], mbv[:, :, b], badv[:, l, :], ALU.add, [mb, badaT], [modT])

        def modcol(l, i, dc, b):
            return modv[:, l, i * 8 + dc, b:b + 1]

        n3v = norm3.ap[:, 0:L * 24].rearrange("p (l i d) -> p l i d", l=L, i=3)
        gsvv = gsv.ap[:, 0:24].rearrange("p (i d) -> p i d", i=3)
        hgvv = hgv.ap[:, 0:16].rearrange("p (i d) -> p i d", i=2)

        def gc(col):
            return gcols.ap[:, col:col + 1]

        GC_QA = 0
        GC_KVA = L * 3
        GC_X = L * 3 + L * 2

        def norm_mod(l, b, i, tts, dst):
            for k, tt in enumerate(tts):
                sb = bank("stat", [6, 7])
                for dc in range(DC):
                    sq = ring("sq", sq_r)
                    act(sq.ap, hT[dc][tt].ap, AF.Square, [hT[dc][tt]], [sq])
                    mm(sb.ap, ones_bf.ap[:, 0:128], sq.ap, dc == 0, dc == DC - 1, [ones_bf, sq], [sb])
                rs = ring("rstd", rstd_r)
                act(rs.ap, sb.ap, AF.Ln, [sb, epst], [rs], bias=epst.ap[:, 0:1], scale=1.0 / D)
                act(rs.ap, rs.ap, AF.Exp, [rs], [rs], scale=-0.5)
                for dc in range(DC):
                    ta = ring("tA", tA_r)
                    stt(ta.ap, hT[dc][tt].ap, gsvv[:, i, dc:dc + 1], rs.ap, ALU.mult, ALU.mult,
                        [hT[dc][tt], gsv, rs], [ta])
                    act(dst[dc][k].ap, ta.ap, AF.Identity, [ta, modT], [dst[dc][k]], bias=modcol(l, 3 * i, dc, b), scale=1.0)

        def ffn(l, b, which):
            w_in_dram = (w_f1i_d if which == 0 else w_f2i_d)[l]
            w_out_dram = (w_f1o_d if which == 0 else w_f2o_d)[l]
            i = 0 if which == 0 else 2
            hi = 0 if which == 0 else 1
            norm_mod(l, b, i, list(range(NT)), uT)
            quarters = [list(range(0, 6)), list(range(6, 12)), list(range(12, 17)), list(range(17, 22))]
            for chunks in quarters:
                for jj, j in enumerate(chunks):
                    slot, v = wload([(w_in_dram[:, j * 128:(j + 1) * 128], 128),
                                     (w_in_dram[:, FF + j * 128:FF + (j + 1) * 128], 128)], 8)
                    if jj == min(2, len(chunks) - 1):
                        for j2i, j2 in enumerate(chunks):
                            dma_pool(wout[j2i].ap, w_out_dram[j2 * 128:(j2 + 1) * 128, :], [], [wout[j2i]], ("wo", j2i))
                    for tt in range(NT):
                        pa = bank("A", [0, 1])
                        pb = bank("B", [2, 3])
                        for kc in range(8):
                            mm(pa.ap, v[:, kc, 0:128], uT[kc][tt].ap, kc == 0, kc == 7, [slot, uT[kc][tt]], [pa])
                        for kc in range(8):
                            mm(pb.ap, v[:, kc, 128:256], uT[kc][tt].ap, kc == 0, kc == 7, [slot, uT[kc][tt]], [pb])
                        tb = ring("tB", tB_r)
                        act(tb.ap, pa.ap, AF.Silu, [pa], [tb])
                        tt_(yT[jj][tt].ap, tb.ap, pb.ap, ALU.mult, [tb, pb], [yT[jj][tt]])
                for tt in range(NT):
                    for dc in range(DC):
                        po = bank("O", [4, 5])
                        for jj in range(len(chunks)):
                            mm(po.ap, wout[jj].ap[:, dc * 128:(dc + 1) * 128], yT[jj][tt].ap, jj == 0, jj == len(chunks) - 1,
                               [wout[jj], yT[jj][tt]], [po])
                        stt(hT[dc][tt].ap, po.ap, hgvv[:, hi, dc:dc + 1], hT[dc][tt].ap, ALU.mult, ALU.add,
                            [po, hgv, hT[dc][tt]], [hT[dc][tt]])

        def rstd_from(sb_ap, np_, dim, reads):
            rs = ring("rstd", rstd_r)
            act(rs.ap[0:np_], sb_ap, AF.Ln, reads + [epst], [rs], bias=epst.ap[0:np_, 0:1], scale=1.0 / dim)
            act(rs.ap[0:np_], rs.ap[0:np_], AF.Exp, [rs], [rs], scale=-0.5)
            return rs

        def finalize_rope(pz, pzs, np_, ones_ap, dim, gcol, gswcol, tt, dst, dst2=None):
            sq = ring("sq", sq_r)
            act(sq.ap[0:np_], pz.ap[0:np_], AF.Square, [pz], [sq])
            sb = bank("stat", [6, 7])
            mm(sb.ap[0:np_], ones_ap, sq.ap[0:np_], True, True, [ones_bf, blk_bf, sq], [sb])
            rs = rstd_from(sb.ap[0:np_], np_, dim, [sb])
            ta = ring("tA", tA_r)
            tb = ring("tB", tB_r)
            cs = slice(tt * TT, (tt + 1) * TT)
            stt(ta.ap[0:np_], pz.ap[0:np_], gc(gcol)[0:np_], tabC.ap[0:np_, cs], ALU.mult, ALU.mult, [pz, gcols, tabC], [ta])
            stt(tb.ap[0:np_], pzs.ap[0:np_], gc(gswcol)[0:np_], tabS.ap[0:np_, cs], ALU.mult, ALU.mult, [pzs, gcols, tabS], [tb])
            tt_(ta.ap[0:np_], ta.ap[0:np_], tb.ap[0:np_], ALU.add, [ta, tb], [ta])
            if dst2 is None:
                tt_(dst.ap[0:np_], ta.ap[0:np_], rs.ap[0:np_], ALU.mult, [ta, rs], [dst])
            else:
                tt_(dst.ap[0:64], ta.ap[0:64], rs.ap[0:64], ALU.mult, [ta, rs], [dst])
                tt_(dst2.ap[64:128], ta.ap[64:128], rs.ap[64:128], ALU.mult, [ta, rs], [dst2])

        def attention(kt_tiles, qt_tiles, prow, dk, v_of_chunk, scale, nb_ap, o_tiles, half):
            for tt in range(NT):
                po = bank("O", [4, 5])
                pairs = {}
                pts = {}

                def s_group(g):
                    pr = ring("spair", [0, 1])
                    pairs[g] = pr
                    for j in range(2):
                        c = 2 * g + j
                        sbk = PS[2 * pr + j]
                        kt = kt_tiles[c // 4]
                        mm(sbk.ap, kt.ap[prow, (c % 4) * 128:(c % 4 + 1) * 128], qt_tiles[tt].ap[prow, :], True, True,
                           [kt, qt_tiles[tt]], [sbk])
                    pt = ring("pt", PT)
                    pts[g] = pt
                    act(pt.ap, psall[:, pr * 1024:(pr + 1) * 1024], AF.Exp, [PS[2 * pr], PS[2 * pr + 1], nbias], [pt],
                        bias=nb_ap, scale=scale)

                def pv_group(g):
                    for j in range(2):
                        c = 2 * g + j
                        vt, vap = v_of_chunk(c)
                        mm(po.ap, vap, pts[g].ap[:, j * 512:(j + 1) * 512], c == 0, c == 15, [vt, pts[g]], [po])

                s_group(0)
                for g in range(8):
                    if g + 1 < 8:
                        s_group(g + 1)
                    pv_group(g)
                orow = slice(half * 64, half * 64 + 64)
                srow = slice((1 - half) * 64, (1 - half) * 64 + 64)
                P.add("dve", (lambda a=rinv.ap[srow], b=po.ap[srow]: lambda e: e.reciprocal(a, b))(), [po], [rinv])
                tt_(o_tiles[tt][half].ap, po.ap[orow], rinv.ap[srow], ALU.mult, [po, rinv], [o_tiles[tt][half]])

        def mixer(l, b):
            wi = w_in_d[l]
            gx = GC_X + 8 * l
            norm_mod(l, b, 1, list(range(NT)), uT)
            dma_sp(tabC.ap, tabs_d[0], [], [tabC], "tabC")
            dma_sp(tabS.ap, tabs_d[1], [], [tabS], "tabS")
            for g in range(2):
                for hf in range(2):
                    for tt in range(NT):
                        zr = slice((1 - hf) * 64, (1 - hf) * 64 + 64)
                        P.add("dve", (lambda a=gkz[g][hf][tt].ap[zr]: lambda e: e.memset(a, 0.0))(), [], [gkz[g][hf][tt]])
            for g in range(2):
                slot, v = wload([(w_gkrep_d[l][:, g * 128:(g + 1) * 128], 128), (w_gkswrep_d[l][:, g * 128:(g + 1) * 128], 128)], 8)
                for tt in range(NT):
                    pz = bank("Z", [0, 1, 2])
                    pzs = bank("Zs", [3, 4, 5])
                    for kc in range(8):
                        mm(pz.ap, v[:, kc, 0:128], uT[kc][tt].ap, kc == 0, kc == 7, [slot, uT[kc][tt]], [pz])
                    for kc in range(8):
                        mm(pzs.ap, v[:, kc, 128:256], uT[kc][tt].ap, kc == 0, kc == 7, [slot, uT[kc][tt]], [pzs])
                    finalize_rope(pz, pzs, 128, blk_bf.ap[:, 0:128], 64, gx + 6, gx + 7, tt, gkz[g][0][tt], gkz[g][1][tt])
            for p in range(4):
                slot, v = wload([(wi[:, 672 + p * 128:672 + (p + 1) * 128], 128), (w_gqsw_d[l][:, p * 128:(p + 1) * 128], 128)], 8)
                for tt in range(NT):
                    pz = bank("Z", [0, 1, 2])
                    pzs = bank("Zs", [3, 4, 5])
                    for kc in range(8):
                        mm(pz.ap, v[:, kc, 0:128], uT[kc][tt].ap, kc == 0, kc == 7, [slot, uT[kc][tt]], [pz])
                    for kc in range(8):
                        mm(pzs.ap, v[:, kc, 128:256], uT[kc][tt].ap, kc == 0, kc == 7, [slot, uT[kc][tt]], [pzs])
                    finalize_rope(pz, pzs, 128, blk_bf.ap[:, 0:128], 64, gx + 4, gx + 5, tt, gq[p][tt])
            for c4 in range(4):
                for blkk in (0, 2, 4):
                    P.add("dve", (lambda a=gv[c4].ap[:, :, blkk, :]: lambda e: e.memset(a, 1.0))(), [], [gv[c4]])
            slot, v = wload([(wi[:, 1312:1440], 128)], 8)
            for c4 in range(4):
                pv = bank("Z", [0, 1, 2])
                for cc in range(4):
                    for kc in range(8):
                        mm(pv.ap[:, cc * 128:(cc + 1) * 128], uT[kc][c4].ap[:, cc * 128:(cc + 1) * 128], v[:, kc, 0:128],
                           kc == 0, kc == 7, [slot, uT[kc][c4]], [pv])
                pvv = pv.ap.rearrange("p (c g d) -> p c g d", c=4, g=2)
                for g in range(2):
                    dcopy(gv[c4].ap[:, :, 1 + 2 * g, :], pvv[:, :, g, :], [pv], [gv[c4]])
            if stop_stage == 31:
                return
            nb_g = nbias.ap[:, l * 2 + 1:l * 2 + 2]
            for p in range(4):
                g = p // 2
                for half in range(2):
                    prow = slice(half * 64, half * 64 + 64)
                    voff = (1 + 2 * g) * 64 if half == 0 else (2 * g) * 64

                    def v_of_chunk(c, voff=voff):
                        t = gv[c // 4]
                        fl = t.ap.rearrange("p c g d -> p c (g d)")
                        return t, fl[:, c % 4, voff:voff + 128]
                    attention(gkz[g][half], gq[p], slice(0, 128), 64, v_of_chunk, 64.0 ** -0.5, nb_g, og[p], half)
            if stop_stage == 32:
                return
            dma_sp(tabC.ap, tabs_d[2], [], [tabC], "tabC")
            dma_sp(tabS.ap, tabs_d[3], [], [tabS], "tabS")
            slA, vA = wload([(wi[:, 0:256], 256)], 8)
            slB, vB = wload([(wi[:, 256:512], 256)], 8)
            slC, vC = wload([(wi[:, 512:640], 128)], 8)
            srcs = [(slA, vA, 0), (slA, vA, 128), (slB, vB, 0), (slB, vB, 128), (slC, vC, 0)]
            for tt in range(NT):
                pzq = []
                for m in range(3):
                    sl, vv, c0 = srcs[m]
                    pz = bank("Zq", [0, 1, 2, 3, 4, 5])
                    pzq.append(pz)
                    for kc in range(8):
                        mm(pz.ap, vv[:, kc, c0:c0 + 128], uT[kc][tt].ap, kc == 0, kc == 7, [sl, uT[kc][tt]], [pz])
                sb = bank("stat", [6, 7])
                for m in range(3):
                    sq = ring("sq", sq_r)
                    act(sq.ap, pzq[m].ap, AF.Square, [pzq[m]], [sq])
                    mm(sb.ap, ones_bf.ap[:, 0:128], sq.ap, m == 0, m == 2, [ones_bf, sq], [sb])
                rs = rstd_from(sb.ap, 128, 384.0, [sb])
                for m in range(3):
                    stt(qn[m][tt].ap, pzq[m].ap, gc(GC_QA + l * 3 + m), rs.ap, ALU.mult, ALU.mult, [pzq[m], gcols, rs], [qn[m][tt]])
                pzk = []
                for m in range(2):
                    sl, vv, c0 = srcs[3 + m]
                    pz = bank("Zq", [0, 1, 2, 3, 4, 5])
                    pzk.append(pz)
                    for kc in range(8):
                        mm(pz.ap, vv[:, kc, c0:c0 + 128], uT[kc][tt].ap, kc == 0, kc == 7, [sl, uT[kc][tt]], [pz])
                sb = bank("stat", [6, 7])
                for m in range(2):
                    sq = ring("sq", sq_r)
                    act(sq.ap, pzk[m].ap, AF.Square, [pzk[m]], [sq])
                    mm(sb.ap, ones_bf.ap[:, 0:128], sq.ap, m == 0, m == 1, [ones_bf, sq], [sb])
                rs = rstd_from(sb.ap, 128, 256.0, [sb])
                for m in range(2):
                    stt(kvn[m][tt].ap, pzk[m].ap, gc(GC_KVA + l * 2 + m), rs.ap, ALU.mult, ALU.mult, [pzk[m], gcols, rs], [kvn[m][tt]])
            slot, v = wload([(w_krpad_d[l], 96), (w_krswpad_d[l], 96)], 8)
            kr = slice(64, 96)
            for tt in range(NT):
                pz = bank("Zq", [0, 1, 2, 3, 4, 5])
                pzs = bank("Zq", [0, 1, 2, 3, 4, 5])
                for kc in range(8):
                    mm(pz.ap[0:96], v[:, kc, 0:96], uT[kc][tt].ap, kc == 0, kc == 7, [slot, uT[kc][tt]], [pz])
                for kc in range(8):
                    mm(pzs.ap[0:96], v[:, kc, 96:192], uT[kc][tt].ap, kc == 0, kc == 7, [slot, uT[kc][tt]], [pzs])
                act(sqk[tt].ap, pz.ap[kr], AF.Square, [pz], [sqk[tt]])
                ta = ring("tA", tA_r)
                tb = ring("tB", tB_r)
                cs = slice(tt * TT, (tt + 1) * TT)
                stt(ta.ap[kr], pz.ap[kr], gc(gx + 2)[kr], tabC.ap[kr, cs], ALU.mult, ALU.mult, [pz, gcols, tabC], [ta])
                stt(tb.ap[kr], pzs.ap[kr], gc(gx + 3)[kr], tabS.ap[kr, cs], ALU.mult, ALU.mult, [pzs, gcols, tabS], [tb])
                tt_(gbk[tt].ap, ta.ap[kr], tb.ap[kr], ALU.add, [ta, tb], [gbk[tt]])
            if stop_stage == 33:
                return
            nb_m = nbias.ap[:, l * 2:l * 2 + 1]
            r96 = slice(0, 96)
            r64 = slice(0, 64)
            for tt in range(NT):
                P.add("dve", (lambda a=kThF[tt].ap[96:128]: lambda e: e.memset(a, 0.0))(), [], [kThF[tt]])
                P.add("dve", (lambda a=qThF[tt].ap[96:128]: lambda e: e.memset(a, 0.0))(), [], [qThF[tt]])
            for h in range(8):
                p, half = h // 2, h % 2
                s = wctr[0] % NWS
                wctr[0] += 1
                slot = wslot[s]
                vq = slot.ap[:, 0:288].rearrange("p (k c) -> p k c", k=3)
                vqs = slot.ap[:, 288:576].rearrange("p (k c) -> p k c", k=3)
                vkv = slot.ap[:, 576:832].rearrange("p (k c) -> p k c", k=2)
                dma_pool(vq, w_uq_d[l][:, h * 96:(h + 1) * 96].rearrange("(k p) c -> p k c", p=128), [], [slot], ("w", s))
                dma_pool(vqs, w_uqsw_d[l][:, h * 96:(h + 1) * 96].rearrange("(k p) c -> p k c", p=128), [], [slot], ("w", s))
                dma_pool(vkv, w_ukv_d[l][:, h * 128:(h + 1) * 128].rearrange("(k p) c -> p k c", p=128), [], [slot], ("w", s))
                pks = []
                for tt in range(NT):
                    pk = PS[tt]
                    pks.append(pk)
                    for kc in range(2):
                        mm(pk.ap[r64], vkv[:, kc, 0:64], kvn[kc][tt].ap, kc == 0, kc == 1, [slot, kvn[kc][tt]], [pk])
                for blkk in (0, 2):
                    P.add("dve", (lambda a=Vh.ap[:, :, blkk, :]: lambda e: e.memset(a, 1.0))(), [], [Vh])
                for c8 in range(2):
                    pv = PS[4 + c8]
                    for cc in range(8):
                        c = c8 * 8 + cc
                        for kc in range(2):
                            mm(pv.ap[:, cc * 64:(cc + 1) * 64], kvn[kc][c // 4].ap[:, (c % 4) * 128:(c % 4 + 1) * 128],
                               vkv[:, kc, 64:128], kc == 0, kc == 1, [slot, kvn[kc][c // 4]], [pv])
                    dcopy(Vh.ap[:, c8 * 8:(c8 + 1) * 8, 1, :], pv.ap.rearrange("p (c d) -> p c d", c=8), [pv], [Vh])
                for tt in range(NT):
                    pk = pks[tt]
                    P.add("dve", (lambda o=gbn[tt].ap, i_=pk.ap[r64], sc=gc(gx + 2)[r64]:
                                  lambda e: e.tensor_scalar(o, i_, sc, None, ALU.mult))(), [pk, gcols], [gbn[tt]])
                    act(sqn[tt].ap, pk.ap[r64], AF.Square, [pk], [sqn[tt]])
                for tt in range(NT):
                    sb = bank("stat", [6, 7])
                    sqfull = view(OFF_R1 + 28672 + tt * TT * 2, TT * 2, BF16)[r96]
                    mm(sb.ap[r96], ones_bf.ap[r96, 0:96], sqfull, True, True, [ones_bf, sqn[tt], sqk[tt]], [sb])
                    rs = rstd_from(sb.ap[r96], 96, 96.0, [sb])
                    gfull = view(OFF_R1 + 20480 + tt * TT * 4, TT * 4, F32)[r96]
                    tt_(kTh[tt].ap, gfull, rs.ap[r96], ALU.mult, [gbn[tt], gbk[tt], rs], [kTh[tt]])
                pqs_l = {}

                def q_mms(tt):
                    pq = PS[tt]
                    pqs = PS[4 + tt % 2]
                    pqs_l[tt] = (pq, pqs)
                    for kc in range(3):
                        mm(pq.ap[r96], vq[:, kc, :], qn[kc][tt].ap, kc == 0, kc == 2, [slot, qn[kc][tt]], [pq])
                    for kc in range(3):
                        mm(pqs.ap[r96], vqs[:, kc, :], qn[kc][tt].ap, kc == 0, kc == 2, [slot, qn[kc][tt]], [pqs])

                q_mms(0)
                q_mms(1)
                for tt in range(NT):
                    pq, pqs = pqs_l[tt]
                    finalize_rope(pq, pqs, 96, ones_bf.ap[r96, 0:96], 96.0, gx + 0, gx + 1, tt, qTh[tt])
                    if tt + 2 < NT:
                        q_mms(tt + 2)
                voff = 64 if half == 0 else 0

                def v_of_chunk(c, voff=voff):
                    fl = Vh.ap.rearrange("p c g d -> p c (g d)")
                    return Vh, fl[:, c, voff:voff + 128]
                attention(kThF, qThF, slice(0, 128), 96, v_of_chunk, 96.0 ** -0.5, nb_m, om[p], half)
            if stop_stage == 34:
                return
            def merge_norm(hf):
                norm_mod(l, b, 1, [hf * 2, hf * 2 + 1], upT)

            def merge_gates(hf):
                tts = [hf * 2, hf * 2 + 1]
                for dch in range(DC):
                    slG, vG = wload([(wi[:, 1440 + dch * 128:1440 + (dch + 1) * 128], 128),
                                     (wi[:, 2464 + dch * 128:2464 + (dch + 1) * 128], 128)], 8)
                    slB_, vBr = wload([(w_bm_d[l][:, dch * 128:(dch + 1) * 128], 128),
                                       (w_bg_d[l][:, dch * 128:(dch + 1) * 128], 128)], 4)
                    for t2 in range(2):
                        tt = tts[t2]
                        pg1 = bank("A", [0, 1])
                        pg2 = bank("B", [2, 3])
                        pb1 = bank("O", [4, 5])
                        pb2 = bank("stat", [6, 7])
                        for kc in range(8):
                            mm(pg1.ap, vG[:, kc, 0:128], upT[kc][t2].ap, kc == 0, kc == 7, [slG, upT[kc][t2]], [pg1])
                        for kc in range(8):
                            mm(pg2.ap, vG[:, kc, 128:256], upT[kc][t2].ap, kc == 0, kc == 7, [slG, upT[kc][t2]], [pg2])
                        for p in range(4):
                            ofull = view(OFF_UO + (p * T + tt * TT) * 2, TT * 2, BF16)
                            mm(pb1.ap, vBr[:, p, 0:128], ofull, p == 0, p == 3, [slB_, om[p][tt][0], om[p][tt][1]], [pb1])
                        for p in range(4):
                            ofull = view(OFF_OG + (p * T + tt * TT) * 2, TT * 2, BF16)
                            mm(pb2.ap, vBr[:, p, 128:256], ofull, p == 0, p == 3, [slB_, og[p][tt][0], og[p][tt][1]], [pb2])
                        ta = ring("tA", tA_r)
                        tb = ring("tB", tB_r)
                        act(ta.ap, pg1.ap, AF.Sigmoid, [pg1], [ta])
                        act(tb.ap, pg2.ap, AF.Sigmoid, [pg2], [tb])
                        tt_(ta.ap, ta.ap, pb1.ap, ALU.mult, [ta, pb1], [ta])
                        tt_(tb.ap, tb.ap, pb2.ap, ALU.mult, [tb, pb2], [tb])
                        tt_(mT[dch][t2].ap, ta.ap, tb.ap, ALU.add, [ta, tb], [mT[dch][t2]])

            def merge_out(hf):
                tts = [hf * 2, hf * 2 + 1]
                for dc in range(DC):
                    slO, vO = wload([(w_out_d[l][:, dc * 128:(dc + 1) * 128], 128)], 8)
                    for t2 in range(2):
                        tt = tts[t2]
                        po = bank("O", [4, 5])
                        for k in range(8):
                            mm(po.ap, vO[:, k, :], mT[k][t2].ap, k == 0, k == 7, [slO, mT[k][t2]], [po])
                        stt(hT[dc][tt].ap, po.ap, modcol(l, 5, dc, b), hT[dc][tt].ap, ALU.mult, ALU.add,
                            [po, modT, hT[dc][tt]], [hT[dc][tt]])

            merge_norm(0)
            merge_gates(0)
            merge_norm(1)
            merge_out(0)
            merge_gates(1)
            merge_out(1)

        for b in range(nseq):
            for ti in range(16):
                st = xs[ti % 2]
                dma_act(st.ap, x_d[b, ti * 128:(ti + 1) * 128, :], [], [st], ("xs", ti % 2))
                tt, k = ti // 4, ti % 4
                for hfb in range(2):
                    pb_ = bank("tr", [6, 7])
                    for d4 in range(4):
                        dc = hfb * 4 + d4
                        P.add("pe", (lambda o=pb_.ap[:, d4 * 128:(d4 + 1) * 128], i_=st.ap[:, dc * 128:(dc + 1) * 128]:
                                     lambda e: e.transpose(o, i_, ident.ap[:, 0:128]))(), [st, ident], [pb_])
                    hv = view(OFF_H, 65536, F32).rearrange("p (d t) -> p d t", d=DC)
                    dcopy(hv[:, hfb * 4:hfb * 4 + 4, tt * TT + k * 128:tt * TT + (k + 1) * 128],
                          pb_.ap.rearrange("p (d t) -> p d t", d=4), [pb_], [hT[hfb * 4 + d4][tt] for d4 in range(4)])
            for l in range(depth):
                for i in range(3):
                    stt(gsvv[:, i, :], modv[:, l, (3 * i + 1) * 8:(3 * i + 2) * 8, b], 1.0, n3v[:, l, i, :], ALU.add, ALU.mult,
                        [modT, norm3], [gsv])
                for hi_, i in ((0, 0), (1, 2)):
                    P.add("dve", (lambda o=hgvv[:, hi_, :], a=modv[:, l, (3 * i + 2) * 8:(3 * i + 3) * 8, b]:
                                  lambda e: e.tensor_scalar(o, a, 0.5, None, ALU.mult))(), [modT], [hgv])
                if stop_stage == 1:
                    break
                ffn(l, b, 0)
                if stop_stage == 2:
                    break
                mixer(l, b)
                if stop_stage == 3 or stop_stage > 30:
                    break
                ffn(l, b, 1)
            hv = view(OFF_H, 65536, F32).rearrange("p (d t) -> p d t", d=DC)
            for ti in range(16):
                st = ys[ti % 2]
                tt, k = ti // 4, ti % 4
                for hfb in range(2):
                    pb_ = bank("tr", [6, 7])
                    for d4 in range(4):
                        dc = hfb * 4 + d4
                        P.add("pe", (lambda o=pb_.ap[:, d4 * 128:(d4 + 1) * 128],
                                     i_=hT[dc][tt].ap[:, k * 128:(k + 1) * 128]:
                                     lambda e: e.transpose(o, i_, ident.ap[:, 0:128]))(), [hT[dc][tt], ident], [pb_])
                    dcopy(st.ap[:, hfb * 512:(hfb + 1) * 512], pb_.ap, [pb_], [st])
                dma_sp(out_d[b, ti * 128:(ti + 1) * 128, :], st.ap, [st], [], ("ys", ti % 2))
        P.add("sp", lambda e: e.nop(), [], [ys[0], ys[1]])
        P.emit(nc, es)
    return nc


def _rope_tables():
    f32 = np.float32

    def tables(pos, dim):
        inv = (np.float32(10000.0) ** (-(np.arange(0, dim, 2, dtype=f32)) / f32(dim))).astype(f32)
        ang = pos.astype(f32)[:, None] * inv[None, :]
        return np.cos(ang).astype(f32), np.sin(ang).astype(f32)

    t = np.arange(T)
    row = t // 64
    col = t % 64
    pc, ps = tables(t, 32)
    rc, rs = tables(row, 32)
    cc, cs = tables(col, 32)
    tabs = np.zeros((4, 128, T), f32)
    Cg = np.concatenate([rc.T, rc.T, cc.T, cc.T], axis=0)
    Sg = np.concatenate([-rs.T, rs.T, -cs.T, cs.T], axis=0)
    tabs[0] = np.concatenate([Cg, Cg], axis=0)
    tabs[1] = np.concatenate([Sg, Sg], axis=0)
    tabs[2, 0:64] = 1.0
    tabs[2, 64:96] = np.concatenate([pc.T, pc.T], axis=0)
    tabs[3, 64:96] = np.concatenate([-ps.T, ps.T], axis=0)
    return tabs


def _swap_idx(n, half):
    idx = np.arange(n)
    blk = idx // (2 * half)
    r = idx % (2 * half)
    return blk * 2 * half + (r + half) % (2 * half)


def _prep_shared(inp):
    f32 = np.float32
    A = lambda k: np.ascontiguousarray(np.asarray(inp[k], dtype=f32))
    w_in = A("w_in")
    sh = {}
    sh["w_ada"] = A("w_ada")
    sh["b_adaT"] = np.ascontiguousarray(A("b_ada").reshape(L, 72, 128).transpose(0, 2, 1))
    n3 = np.stack([A("norm_ffn1"), A("norm_mix"), A("norm_ffn2")], axis=1)
    sh["norm3"] = np.ascontiguousarray(n3.reshape(L, 3, 8, 128).transpose(3, 0, 1, 2).reshape(128, L * 24))
    for k in ("w_ffn1_in", "w_ffn1_out", "w_ffn2_in", "w_ffn2_out", "mla_w_uq", "mla_w_ukv", "w_branch_mla",
              "w_branch_gqa", "w_out"):
        sh[k] = A(k)
    sh["w_in"] = w_in
    sw16 = _swap_idx(512, 16)
    gq = w_in[:, :, 672:1184]
    sh["w_gq_sw"] = np.ascontiguousarray(gq[:, :, sw16])
    gkc = w_in[:, :, 1184:1312]
    gks = gkc[:, :, _swap_idx(128, 16)]
    rep = lambda a: np.ascontiguousarray(np.concatenate([a[:, :, 0:64], a[:, :, 0:64], a[:, :, 64:128], a[:, :, 64:128]], axis=2))
    sh["w_gk_rep"] = rep(gkc)
    sh["w_gk_sw_rep"] = rep(gks)
    kr = w_in[:, :, 640:672]
    z64 = np.zeros((L, D, 64), f32)
    sh["w_kr_pad"] = np.ascontiguousarray(np.concatenate([z64, kr], axis=2))
    sh["w_kr_sw_pad"] = np.ascontiguousarray(np.concatenate([z64, kr[:, :, _swap_idx(32, 16)]], axis=2))
    uq = A("mla_w_uq").reshape(L, 384, 8, 96)
    uqsw = np.zeros_like(uq)
    uqsw[:, :, :, 64:96] = uq[:, :, :, 64:96][:, :, :, _swap_idx(32, 16)]
    sh["w_uq_sw"] = np.ascontiguousarray(uqsw.reshape(L, 384, 768))
    NG = L * 3 + L * 2 + 8 * L
    gcols = np.zeros((128, NG), f32)
    qa = A("mla_q_a_norm")
    kva = A("mla_kv_a_norm")
    qkq = A("mla_qk_q_norm")
    qkk = A("mla_qk_k_norm")
    gqn = A("gqa_q_norm")
    gkn = A("gqa_k_norm")
    s32 = _swap_idx(32, 16)
    s64 = _swap_idx(64, 16)
    for l in range(L):
        gcols[:, l * 3:(l + 1) * 3] = qa[l].reshape(3, 128).T
        gcols[:, L * 3 + l * 2:L * 3 + (l + 1) * 2] = kva[l].reshape(2, 128).T
        gx = L * 3 + L * 2 + 8 * l
        gcols[0:96, gx + 0] = qkq[l]
        gcols[64:96, gx + 1] = qkq[l][64:96][s32]
        gcols[0:96, gx + 2] = qkk[l]
        gcols[64:96, gx + 3] = qkk[l][64:96][s32]
        gcols[:, gx + 4] = np.concatenate([gqn[l], gqn[l]])
        gcols[:, gx + 5] = np.concatenate([gqn[l][s64], gqn[l][s64]])
        gcols[:, gx + 6] = np.concatenate([gkn[l], gkn[l]])
        gcols[:, gx + 7] = np.concatenate([gkn[l][s64], gkn[l][s64]])
    sh["gcols"] = gcols
    grow = np.zeros((1, L * 4 * 96), f32)
    for l in range(L):
        grow[0, (l * 4 + 0) * 96:(l * 4 + 0) * 96 + 96] = qkq[l]
        grow[0, (l * 4 + 1) * 96:(l * 4 + 1) * 96 + 96] = qkk[l]
        grow[0, (l * 4 + 2) * 96:(l * 4 + 2) * 96 + 64] = gqn[l]
        grow[0, (l * 4 + 3) * 96:(l * 4 + 3) * 96 + 64] = gkn[l]
    sh["grow"] = grow
    sh["tabs"] = _rope_tables()
    sh["ident"] = np.eye(128, dtype=f32)
    return sh


_PROG_CACHE = {}


def _get_prog(nseq, depth=L, stop_stage=0):
    key = (nseq, depth, stop_stage)
    if key not in _PROG_CACHE:
        _PROG_CACHE[key] = build_program(nseq, depth, stop_stage)
    return _PROG_CACHE[key]


def kernel(**inputs):
    x = np.asarray(inputs["x"], dtype=np.float32)
    c = np.asarray(inputs["c"], dtype=np.float32)
    B = x.shape[0]
    nseq = B // NCORES
    sh = _prep_shared(inputs)
    in_maps = []
    for i in range(NCORES):
        m = dict(sh)
        m["x"] = np.ascontiguousarray(x[i * nseq:(i + 1) * nseq])
        m["cT"] = np.ascontiguousarray(c[i * nseq:(i + 1) * nseq].T)
        in_maps.append(m)
    nc = _get_prog(nseq)
    res = run_bass_kernel_spmd(nc, in_maps, core_ids=list(range(NCORES)))
    return np.concatenate([np.asarray(r["out"]) for r in res.results], axis=0).astype(np.float32)
```

```python
import numpy as np
from contextlib import ExitStack
import concourse.bass as bass
import concourse.mybir as mybir
from concourse.bass_utils import run_bass_kernel_spmd

F32 = mybir.dt.float32
BF16 = mybir.dt.bfloat16
AF = mybir.ActivationFunctionType
ALU = mybir.AluOpType
AX = mybir.AxisListType

NCORES = 8
L = 2
D = 1024
DC = 8
T = 2048
TT = 512
NT = 4
FF = 2816
FC = 22
DIN = 3488
EPS = 1e-6
SAME_ENG_RAW_SYNC = True


class Tile:
    __slots__ = ("name", "ap", "lo", "hi", "plo", "phi", "lw", "rd", "ov", "ps")

    def __init__(self, name, ap, lo, hi, plo, phi):
        self.name, self.ap, self.lo, self.hi, self.plo, self.phi = name, ap, lo, hi, plo, phi
        self.ps = False
        self.lw = None
        self.rd = {}
        self.ov = []


class Op:
    __slots__ = ("idx", "eng", "fn", "deps", "dma_key", "dma_n", "sig_needed", "sig", "group")

    def __init__(self, idx, eng, fn, deps, dma_key, group):
        self.idx, self.eng, self.fn, self.deps, self.dma_key, self.group = idx, eng, fn, deps, dma_key, group
        self.dma_n = 0
        self.sig_needed = False
        self.sig = 0


class Prog:
    def __init__(self):
        self.ops = []
        self.tiles = []
        self.dma_cnt = {}
        self.space_tiles = {}

    def tile(self, name, ap, space, lo, hi, plo=0, phi=128):
        t = Tile(name, ap, lo, hi, plo, phi)
        t.ps = (space == "ps")
        lst = self.space_tiles.setdefault(space, [])
        for o in lst:
            if o.lo < hi and lo < o.hi and o.plo < phi and plo < o.phi:
                t.ov.append(o)
                o.ov.append(t)
        lst.append(t)
        return t

    def _need(self, a, eng, is_dma, kind):
        if a.dma_key is not None:
            return True
        if (not is_dma) and a.eng == eng:
            return SAME_ENG_RAW_SYNC and eng != "pe"
        return True

    def add(self, eng, fn, reads=(), writes=(), dma_key=None, group=False):
        idx = len(self.ops)
        deps = {}
        for t in reads:
            if t.lw is not None:
                deps[t.lw] = 0
            for x in t.ov:
                if x.lw is not None:
                    deps[x.lw] = 0
            if t.ps:
                for rk_, r in t.rd.items():
                    if rk_ != eng and r not in deps:
                        deps[r] = 2
        for t in writes:
            for x in [t] + t.ov:
                if x.lw is not None and x.lw not in deps:
                    deps[x.lw] = 1
                for r in x.rd.values():
                    if r not in deps:
                        deps[r] = 2
        is_dma = dma_key is not None
        fdeps = []
        latest = {}
        for a_idx, kind in deps.items():
            a = self.ops[a_idx]
            if self._need(a, eng, is_dma, kind):
                if a.dma_key is None:
                    if a.eng not in latest or latest[a.eng] < a_idx:
                        latest[a.eng] = a_idx
                else:
                    fdeps.append(a_idx)
        for a_idx in latest.values():
            fdeps.append(a_idx)
            self.ops[a_idx].sig_needed = True
        op = Op(idx, eng, fn, fdeps, dma_key, group)
        if is_dma:
            self.dma_cnt[dma_key] = self.dma_cnt.get(dma_key, 0) + 1
            op.dma_n = self.dma_cnt[dma_key]
        rk = ("d", dma_key) if is_dma else eng
        for t in reads:
            t.rd[rk] = idx
        for t in writes:
            t.lw = idx
            t.rd = {}
        self.ops.append(op)
        return op

    def emit(self, nc, es):
        engs = ["pe", "act", "dve", "pool", "sp"]
        cnt = {e: 0 for e in engs}
        for op in self.ops:
            if op.dma_key is None and op.sig_needed:
                cnt[op.eng] += 1
                op.sig = cnt[op.eng]
        esem = {e: es.enter_context(nc.semaphore("sem_" + e)) for e in engs}
        dsem = {k: es.enter_context(nc.semaphore("dsem_%d" % i)) for i, k in enumerate(sorted(self.dma_cnt, key=str))}
        block = es.enter_context(nc.Block())
        ops = self.ops
        dma_cnt = self.dma_cnt

        def stream(ename):
            def body(eng):
                waited = {}
                for op in ops:
                    if op.eng != ename:
                        continue
                    need = {}
                    for a_idx in op.deps:
                        a = ops[a_idx]
                        if a.dma_key is not None:
                            sem = dsem[a.dma_key]
                            val = 16 * (dma_cnt[a.dma_key] if a.group else a.dma_n)
                        else:
                            sem = esem[a.eng]
                            val = a.sig
                        k = id(sem)
                        if k not in need or need[k][1] < val:
                            need[k] = (sem, val)
                    for k, (sem, val) in need.items():
                        if waited.get(k, 0) < val:
                            eng.wait_ge(sem, val)
                            waited[k] = val
                    inst = op.fn(eng)
                    if op.dma_key is not None:
                        inst.then_inc(dsem[op.dma_key], 16)
                    elif op.sig_needed:
                        inst.then_inc(esem[ename], 1)
            return body

        block.tensor(stream("pe"))
        block.scalar(stream("act"))
        block.vector(stream("dve"))
        block.gpsimd(stream("pool"))
        block.sync(stream("sp"))


def build_program(nseq, depth=L, stop_stage=0):
    nc = bass.Bass("TRN2", target_bir_lowering=False)
    P = Prog()

    def din(name, shape):
        return nc.dram_tensor(name, list(shape), F32, kind="ExternalInput").ap()

    x_d = din("x", [nseq, T, D])
    cT_d = din("cT", [D, nseq])
    w_ada_d = din("w_ada", [L, D, 9 * D])
    b_adaT_d = din("b_adaT", [L, 128, 72])
    norm3_d = din("norm3", [128, L * 3 * 8])
    w_f1i_d = din("w_ffn1_in", [L, D, 2 * FF])
    w_f1o_d = din("w_ffn1_out", [L, FF, D])
    w_f2i_d = din("w_ffn2_in", [L, D, 2 * FF])
    w_f2o_d = din("w_ffn2_out", [L, FF, D])
    w_in_d = din("w_in", [L, D, DIN])
    w_gqsw_d = din("w_gq_sw", [L, D, 512])
    w_gkrep_d = din("w_gk_rep", [L, D, 256])
    w_gkswrep_d = din("w_gk_sw_rep", [L, D, 256])
    w_krpad_d = din("w_kr_pad", [L, D, 96])
    w_krswpad_d = din("w_kr_sw_pad", [L, D, 96])
    w_uq_d = din("mla_w_uq", [L, 384, 768])
    w_uqsw_d = din("w_uq_sw", [L, 384, 768])
    w_ukv_d = din("mla_w_ukv", [L, 256, 1024])
    w_bm_d = din("w_branch_mla", [L, 512, D])
    w_bg_d = din("w_branch_gqa", [L, 512, D])
    w_out_d = din("w_out", [L, D, D])
    NG = L * 3 + L * 2 + 8 * L
    gcols_d = din("gcols", [128, NG])
    grow_d = din("grow", [1, L * 4 * 96])
    tabs_d = din("tabs", [4, 128, T])
    ident_d = din("ident", [128, 128])
    out_d = nc.dram_tensor("out", [nseq, T, D], F32, kind="ExternalOutput").ap()

    es = ExitStack()
    with es:
        OFF_H = 0
        OFF_UO = OFF_H + 65536
        OFF_OG = OFF_UO + 32768
        OFF_R1 = OFF_OG + 16384
        OFF_T = OFF_R1 + 34816
        OFF_W = OFF_T + 16384
        NWS = 3
        OFF_S = OFF_W + NWS * 4096
        OFF_P = OFF_S + 16384
        OFF_C = OFF_P + 4096
        ARENA = OFF_C + 8192
        arena = es.enter_context(nc.sbuf_tensor("arena", [128, ARENA // 2], BF16))
        psall = es.enter_context(nc.psum_tensor("psall", [128, 4096], F32))

        def view(off, nbytes, dt):
            a = arena[:, off // 2:(off + nbytes) // 2]
            if dt == F32:
                a = a.bitcast(F32)
            return a

        def mk(name, off, nbytes, dt, plo=0, phi=128, shape=None):
            a = view(off, nbytes, dt)
            if shape is not None:
                if len(shape) == 1:
                    a = a.rearrange("p (a b) -> p a b", a=shape[0])
                elif len(shape) == 2:
                    a = a.rearrange("p (a b c) -> p a b c", a=shape[0], b=shape[1])
            if plo != 0 or phi != 128:
                a = a[plo:phi]
            return P.tile(name, a, "sb", off, off + nbytes, plo, phi)

        PS = [P.tile("ps%d" % i, psall[:, i * 512:(i + 1) * 512], "ps", i * 2048, (i + 1) * 2048) for i in range(8)]

        hT = [[mk("h", OFF_H + (dc * T + tt * TT) * 4, TT * 4, F32) for tt in range(NT)] for dc in range(DC)]
        uT = [[mk("u", OFF_UO + (dc * T + tt * TT) * 2, TT * 2, BF16) for tt in range(NT)] for dc in range(DC)]
        om = [[[mk("om", OFF_UO + (p * T + tt * TT) * 2, TT * 2, BF16, h * 64, h * 64 + 64) for h in range(2)]
               for tt in range(NT)] for p in range(4)]
        og = [[[mk("og", OFF_OG + (p * T + tt * TT) * 2, TT * 2, BF16, h * 64, h * 64 + 64) for h in range(2)]
               for tt in range(NT)] for p in range(4)]
        OFF_PH = OFF_UO + 16384
        kTh = [mk("kTh", OFF_PH + tt * TT * 2, TT * 2, BF16, 0, 96) for tt in range(NT)]
        qTh = [mk("qTh", OFF_PH + 4096 + tt * TT * 2, TT * 2, BF16, 0, 96) for tt in range(NT)]
        kThF = [mk("kThF", OFF_PH + tt * TT * 2, TT * 2, BF16) for tt in range(NT)]
        qThF = [mk("qThF", OFF_PH + 4096 + tt * TT * 2, TT * 2, BF16) for tt in range(NT)]
        Vh = mk("Vh", OFF_PH + 8192, 16 * 192 * 2, BF16, shape=(16, 3))
        gq = [[mk("gq", OFF_R1 + (p * T + tt * TT) * 2, TT * 2, BF16) for tt in range(NT)] for p in range(4)]
        gkz = [[[mk("gkz", OFF_R1 + 16384 + ((g * 2 + hf) * T + tt * TT) * 2, TT * 2, BF16) for tt in range(NT)]
                for hf in range(2)] for g in range(2)]
        gv = [mk("gv", OFF_T + c4 * 2560, 2560, BF16, shape=(4, 5)) for c4 in range(4)]
        qn = [[mk("qn", OFF_R1 + (m * T + tt * TT) * 2, TT * 2, BF16) for tt in range(NT)] for m in range(3)]
        kvn = [[mk("kvn", OFF_R1 + 12288 + (m * T + tt * TT) * 2, TT * 2, BF16) for tt in range(NT)] for m in range(2)]
        gbn = [mk("gbn", OFF_R1 + 20480 + tt * TT * 4, TT * 4, F32, 0, 64) for tt in range(NT)]
        gbk = [mk("gbk", OFF_R1 + 20480 + tt * TT * 4, TT * 4, F32, 64, 96) for tt in range(NT)]
        sqn = [mk("sqn", OFF_R1 + 28672 + tt * TT * 2, TT * 2, BF16, 0, 64) for tt in range(NT)]
        sqk = [mk("sqk", OFF_R1 + 28672 + tt * TT * 2, TT * 2, BF16, 64, 96) for tt in range(NT)]
        yT = [[mk("y", OFF_R1 + (j * T + tt * TT) * 2, TT * 2, BF16) for tt in range(NT)] for j in range(6)]
        upT = [[mk("up", OFF_R1 + (dc * 1024 + t2 * TT) * 2, TT * 2, BF16) for t2 in range(2)] for dc in range(DC)]
        mT = [[mk("mT", OFF_R1 + 16384 + (dc * 1024 + t2 * TT) * 2, TT * 2, BF16) for t2 in range(2)] for dc in range(DC)]
        tabC = mk("tabC", OFF_T, 8192, F32)
        tabS = mk("tabS", OFF_T + 8192, 8192, F32)
        wout = [mk("wout", OFF_T + s * 2048, 2048, BF16) for s in range(6)]
        xs = [mk("xs", OFF_T + s * 4096, 4096, F32) for s in range(2)]
        ys = [mk("ys", OFF_T + 8192 + s * 4096, 4096, F32) for s in range(2)]
        wslot = [mk("w", OFF_W + s * 4096, 4096, BF16) for s in range(NWS)]
        rstd_r = [mk("rstd", OFF_S + s * 2048, 2048, F32) for s in range(2)]
        tA_r = [mk("tA", OFF_S + 4096 + s * 2048, 2048, F32) for s in range(2)]
        tB_r = [mk("tB", OFF_S + 8192 + s * 2048, 2048, F32) for s in range(2)]
        sq_r = [mk("sq", OFF_S + 12288 + s * 1024, 1024, BF16) for s in range(2)]
        rinv = mk("rinv", OFF_S + 14336, 2048, F32)
        PT = [mk("pt", OFF_P + s * 2048, 2048, BF16) for s in range(2)]
        co = [OFF_C]

        def cmk(name, nbytes, dt, shape=None):
            nb = (nbytes + 31) // 32 * 32
            t = mk(name, co[0], nb, dt, shape=None)
            co[0] += nb
            assert co[0] <= ARENA
            return t

        ident = cmk("ident", 512, F32)
        ones_bf = cmk("ones", 256, BF16)
        blk_bf = cmk("blk", 256, BF16)
        ones_f = cmk("ones_f", 512, F32)
        modT = cmk("modT", L * 72 * 4 * 4, F32)
        badaT = cmk("badaT", L * 72 * 4, F32)
        norm3 = cmk("norm3", L * 3 * 8 * 4, F32)
        gcols = cmk("gcols", NG * 4, F32)
        growt = mk("grow", OFF_T, L * 4 * 96 * 4, F32)
        gmax = cmk("gmax", L * 4 * 4, F32)
        nbias = cmk("nbias", L * 2 * 4, F32)
        epst = cmk("eps", 4, F32)
        cTt = cmk("cT", 8 * nseq * 4, F32)
        cact = cmk("cact", 8 * nseq * 2, BF16)
        gsv = cmk("gsv", 3 * 8 * 4, F32)
        hgv = cmk("hgv", 2 * 8 * 4, F32)

        rings = {}

        def ring(name, lst):
            i = rings.get(name, 0)
            rings[name] = i + 1
            return lst[i % len(lst)]

        def bank(name, ids):
            return PS[ring("bank_" + name, ids)]

        def mm(out_ap, lhsT, rhs, start, stop, reads, writes):
            P.add("pe", lambda e: e.matmul(out_ap, lhsT, rhs, start=start, stop=stop), reads, writes)

        def act(out_ap, in_ap, func, reads, writes, bias=None, scale=None):
            kw = {}
            if bias is not None:
                kw["bias"] = bias
            if scale is not None:
                kw["scale"] = scale
            P.add("act", lambda e: e.activation(out=out_ap, in_=in_ap, func=func, **kw), reads, writes)

        def stt(out_ap, in0, scalar, in1, op0, op1, reads, writes):
            P.add("dve", lambda e: e.scalar_tensor_tensor(out_ap, in0, scalar, in1, op0, op1), reads, writes)

        def tt_(out_ap, in0, in1, op, reads, writes):
            P.add("dve", lambda e: e.tensor_tensor(out_ap, in0, in1, op), reads, writes)

        def dcopy(out_ap, in_ap, reads, writes):
            P.add("dve", lambda e: e.tensor_copy(out_ap, in_ap), reads, writes)

        def dma_sp(out_ap, in_ap, reads, writes, key, group=False, slow=False):
            if slow:
                P.add("sp", lambda e: e.dma_start(out=out_ap, in_=in_ap, allow_slow_non_contiguous=True),
                      reads, writes, dma_key=key, group=group)
            else:
                P.add("sp", lambda e: e.dma_start(out=out_ap, in_=in_ap), reads, writes, dma_key=key, group=group)

        def dma_act(out_ap, in_ap, reads, writes, key):
            P.add("act", lambda e: e.dma_start(out=out_ap, in_=in_ap), reads, writes, dma_key=key)

        def dma_pool(out_ap, in_ap, reads, writes, key):
            P.add("pool", lambda e: e.dma_start(out=out_ap, in_=in_ap), reads, writes, dma_key=key)

        wctr = [0]

        def wload(pieces, kcs):
            s = wctr[0] % NWS
            wctr[0] += 1
            slot = wslot[s]
            tot = sum(n for _, n in pieces)
            assert kcs * tot <= 2048
            v = slot.ap[:, 0:kcs * tot].rearrange("p (k c) -> p k c", k=kcs)
            c0 = 0
            for src, n in pieces:
                srcv = src.rearrange("(k p) c -> p k c", p=128)
                dma_pool(v[:, :, c0:c0 + n], srcv, [], [slot], ("w", s))
                c0 += n
            return slot, v

        CK = "const"
        dma_sp(ident.ap[:, 0:128], ident_d, [], [ident], CK, group=True)
        dma_sp(badaT.ap[:, 0:L * 72].rearrange("p (l m) -> p l m", l=L), b_adaT_d.rearrange("l p m -> p l m"),
               [], [badaT], CK, group=True)
        dma_sp(norm3.ap[:, 0:L * 24], norm3_d, [], [norm3], CK, group=True)
        dma_sp(gcols.ap[:, 0:NG], gcols_d, [], [gcols], CK, group=True)
        dma_sp(growt.ap[0:1, 0:L * 4 * 96], grow_d, [], [growt], CK, group=True)
        dma_sp(cTt.ap[:, 0:8 * nseq].rearrange("p (k b) -> p k b", k=8), cT_d.rearrange("(k p) b -> p k b", p=128),
               [], [cTt], CK, group=True, slow=True)
        P.add("dve", lambda e: e.memset(ones_bf.ap[:, 0:128], 1.0), [], [ones_bf])
        P.add("dve", lambda e: e.memset(ones_f.ap[:, 0:128], 1.0), [], [ones_f])
        P.add("dve", lambda e: e.memset(blk_bf.ap[:, 0:128], 0.0), [], [blk_bf])
        P.add("dve", lambda e: e.memset(blk_bf.ap[0:64, 0:64], 1.0), [], [blk_bf])
        P.add("dve", lambda e: e.memset(blk_bf.ap[64:128, 64:128], 1.0), [], [blk_bf])
        P.add("dve", lambda e: e.memset(epst.ap[:, 0:1], EPS), [], [epst])

        gr = growt.ap[0:1, 0:L * 4 * 96].rearrange("p (a b) -> p a b", b=96)
        P.add("dve", lambda e: e.tensor_reduce(gmax.ap[0:1, 0:L * 4], gr, AX.X, ALU.max, apply_absolute_value=True),
              [growt], [gmax])
        for l in range(L):
            for mi, dd in ((0, 96.0), (1, 64.0)):
                a = gmax.ap[0:1, l * 4 + 2 * mi:l * 4 + 2 * mi + 1]
                b = gmax.ap[0:1, l * 4 + 2 * mi + 1:l * 4 + 2 * mi + 2]
                P.add("dve", (lambda a=a, b=b, dd=dd: lambda e: e.scalar_tensor_tensor(
                    a, a, -float(np.sqrt(dd)), b, ALU.mult, ALU.mult))(), [gmax], [gmax])
        bk = PS[7]
        mm(bk.ap[:, 0:L * 4], ones_f.ap[0:1, 0:128], gmax.ap[0:1, 0:L * 4], True, True, [ones_f, gmax], [bk])
        for l in range(L):
            for mi in range(2):
                src = bk.ap[:, l * 4 + 2 * mi:l * 4 + 2 * mi + 1]
                dst = nbias.ap[:, l * 2 + mi:l * 2 + mi + 1]
                dcopy(dst, src, [bk], [nbias])

        act(cact.ap[:, 0:8 * nseq], cTt.ap[:, 0:8 * nseq], AF.Silu, [cTt], [cact])
        cactv = cact.ap[:, 0:8 * nseq].rearrange("p (k b) -> p k b", k=8)
        modv = modT.ap[:, 0:L * 72 * 4].rearrange("p (l m b) -> p l m b", l=L, m=72)
        badv = badaT.ap[:, 0:L * 72].rearrange("p (l m) -> p l m", l=L)
        for l in range(depth):
            mb = PS[6]
            for s in range(36):
                slot, v = wload([(w_ada_d[l][:, s * 256:(s + 1) * 256], 256)], 8)
                for mloc in range(2):
                    m = s * 2 + mloc
                    for kc in range(8):
                        mm(mb.ap[:, m * nseq:(m + 1) * nseq], v[:, kc, mloc * 128:(mloc + 1) * 128], cactv[:, kc, :],
                           kc == 0, kc == 7, [slot, cact], [mb])
            mbv = mb.ap[:, 0:72 * nseq].rearrange("p (m b) -> p m b", b=nseq)
            for b in range(nseq):
                tt_(modv[:, l, :, b], mbv[:, :, b], badv[:, l, :], ALU.add, [mb, badaT], [modT])

        def modcol(l, i, dc, b):
            return modv[:, l, i * 8 + dc, b:b + 1]

        n3v = norm3.ap[:, 0:L * 24].rearrange("p (l i d) -> p l i d", l=L, i=3)
        gsvv = gsv.ap[:, 0:24].rearrange("p (i d) -> p i d", i=3)
        hgvv = hgv.ap[:, 0:16].rearrange("p (i d) -> p i d", i=2)

        def gc(col):
            return gcols.ap[:, col:col + 1]

        GC_QA = 0
        GC_KVA = L * 3
        GC_X = L * 3 + L * 2

        def norm_mod(l, b, i, tts, dst):
            for k, tt in enumerate(tts):
                sb = bank("stat", [6, 7])
                for dc in range(DC):
                    sq = ring("sq", sq_r)
                    act(sq.ap, hT[dc][tt].ap, AF.Square, [hT[dc][tt]], [sq])
                    mm(sb.ap, ones_bf.ap[:, 0:128], sq.ap, dc == 0, dc == DC - 1, [ones_bf, sq], [sb])
                rs = ring("rstd", rstd_r)
                act(rs.ap, sb.ap, AF.Ln, [sb, epst], [rs], bias=epst.ap[:, 0:1], scale=1.0 / D)
                act(rs.ap, rs.ap, AF.Exp, [rs], [rs], scale=-0.5)
                for dc in range(DC):
                    ta = ring("tA", tA_r)
                    stt(ta.ap, hT[dc][tt].ap, gsvv[:, i, dc:dc + 1], rs.ap, ALU.mult, ALU.mult,
                        [hT[dc][tt], gsv, rs], [ta])
                    act(dst[dc][k].ap, ta.ap, AF.Identity, [ta, modT], [dst[dc][k]], bias=modcol(l, 3 * i, dc, b), scale=1.0)

        def ffn(l, b, which):
            w_in_dram = (w_f1i_d if which == 0 else w_f2i_d)[l]
            w_out_dram = (w_f1o_d if which == 0 else w_f2o_d)[l]
            i = 0 if which == 0 else 2
            hi = 0 if which == 0 else 1
            norm_mod(l, b, i, list(range(NT)), uT)
            quarters = [list(range(0, 6)), list(range(6, 12)), list(range(12, 17)), list(range(17, 22))]
            for chunks in quarters:
                for jj, j in enumerate(chunks):
                    slot, v = wload([(w_in_dram[:, j * 128:(j + 1) * 128], 128),
                                     (w_in_dram[:, FF + j * 128:FF + (j + 1) * 128], 128)], 8)
                    if jj == min(2, len(chunks) - 1):
                        for j2i, j2 in enumerate(chunks):
                            dma_pool(wout[j2i].ap, w_out_dram[j2 * 128:(j2 + 1) * 128, :], [], [wout[j2i]], ("wo", j2i))
                    for tt in range(NT):
                        pa = bank("A", [0, 1])
                        pb = bank("B", [2, 3])
                        for kc in range(8):
                            mm(pa.ap, v[:, kc, 0:128], uT[kc][tt].ap, kc == 0, kc == 7, [slot, uT[kc][tt]], [pa])
                        for kc in range(8):
                            mm(pb.ap, v[:, kc, 128:256], uT[kc][tt].ap, kc == 0, kc == 7, [slot, uT[kc][tt]], [pb])
                        tb = ring("tB", tB_r)
                        act(tb.ap, pa.ap, AF.Silu, [pa], [tb])
                        tt_(yT[jj][tt].ap, tb.ap, pb.ap, ALU.mult, [tb, pb], [yT[jj][tt]])
                for tt in range(NT):
                    for dc in range(DC):
                        po = bank("O", [4, 5])
                        for jj in range(len(chunks)):
                            mm(po.ap, wout[jj].ap[:, dc * 128:(dc + 1) * 128], yT[jj][tt].ap, jj == 0, jj == len(chunks) - 1,
                               [wout[jj], yT[jj][tt]], [po])
                        stt(hT[dc][tt].ap, po.ap, hgvv[:, hi, dc:dc + 1], hT[dc][tt].ap, ALU.mult, ALU.add,
                            [po, hgv, hT[dc][tt]], [hT[dc][tt]])

        def rstd_from(sb_ap, np_, dim, reads):
            rs = ring("rstd", rstd_r)
            act(rs.ap[0:np_], sb_ap, AF.Ln, reads + [epst], [rs], bias=epst.ap[0:np_, 0:1], scale=1.0 / dim)
            act(rs.ap[0:np_], rs.ap[0:np_], AF.Exp, [rs], [rs], scale=-0.5)
            return rs

        def finalize_rope(pz, pzs, np_, ones_ap, dim, gcol, gswcol, tt, dst, dst2=None):
            sq = ring("sq", sq_r)
            act(sq.ap[0:np_], pz.ap[0:np_], AF.Square, [pz], [sq])
            sb = bank("stat", [6, 7])
            mm(sb.ap[0:np_], ones_ap, sq.ap[0:np_], True, True, [ones_bf, blk_bf, sq], [sb])
            rs = rstd_from(sb.ap[0:np_], np_, dim, [sb])
            ta = ring("tA", tA_r)
            tb = ring("tB", tB_r)
            cs = slice(tt * TT, (tt + 1) * TT)
            stt(ta.ap[0:np_], pz.ap[0:np_], gc(gcol)[0:np_], tabC.ap[0:np_, cs], ALU.mult, ALU.mult, [pz, gcols, tabC], [ta])
            stt(tb.ap[0:np_], pzs.ap[0:np_], gc(gswcol)[0:np_], tabS.ap[0:np_, cs], ALU.mult, ALU.mult, [pzs, gcols, tabS], [tb])
            tt_(ta.ap[0:np_], ta.ap[0:np_], tb.ap[0:np_], ALU.add, [ta, tb], [ta])
            if dst2 is None:
                tt_(dst.ap[0:np_], ta.ap[0:np_], rs.ap[0:np_], ALU.mult, [ta, rs], [dst])
            else:
                tt_(dst.ap[0:64], ta.ap[0:64], rs.ap[0:64], ALU.mult, [ta, rs], [dst])
                tt_(dst2.ap[64:128], ta.ap[64:128], rs.ap[64:128], ALU.mult, [ta, rs], [dst2])

        def attention(kt_tiles, qt_tiles, prow, dk, v_of_chunk, scale, nb_ap, o_tiles, half):
            for tt in range(NT):
                po = bank("O", [4, 5])
                pairs = {}
                pts = {}

                def s_group(g):
                    pr = ring("spair", [0, 1, 3])
                    pairs[g] = pr
                    for j in range(2):
                        c = 2 * g + j
                        sbk = PS[2 * pr + j]
                        kt = kt_tiles[c // 4]
                        mm(sbk.ap, kt.ap[prow, (c % 4) * 128:(c % 4 + 1) * 128], qt_tiles[tt].ap[prow, :], True, True,
                           [kt, qt_tiles[tt]], [sbk])

                def s_exp(g):
                    pr = pairs[g]
                    pt = ring("pt", PT)
                    pts[g] = pt
                    act(pt.ap, psall[:, pr * 1024:(pr + 1) * 1024], AF.Exp, [PS[2 * pr], PS[2 * pr + 1], nbias], [pt],
                        bias=nb_ap, scale=scale)

                def pv_group(g):
                    for j in range(2):
                        c = 2 * g + j
                        vt, vap = v_of_chunk(c)
                        mm(po.ap, vap, pts[g].ap[:, j * 512:(j + 1) * 512], c == 0, c == 15, [vt, pts[g]], [po])

                s_group(0)
                s_exp(0)
                s_group(1)
                s_exp(1)
                for g in range(8):
                    if g + 2 < 8:
                        s_group(g + 2)
                    pv_group(g)
                    if g + 2 < 8:
                        s_exp(g + 2)
                orow = slice(half * 64, half * 64 + 64)
                srow = slice((1 - half) * 64, (1 - half) * 64 + 64)
                P.add("dve", (lambda a=rinv.ap[srow], b=po.ap[srow]: lambda e: e.reciprocal(a, b))(), [po], [rinv])
                tt_(o_tiles[tt][half].ap, po.ap[orow], rinv.ap[srow], ALU.mult, [po, rinv], [o_tiles[tt][half]])

        def mixer(l, b):
            wi = w_in_d[l]
            gx = GC_X + 8 * l
            norm_mod(l, b, 1, list(range(NT)), uT)
            dma_sp(tabC.ap, tabs_d[0], [], [tabC], "tabC")
            dma_sp(tabS.ap, tabs_d[1], [], [tabS], "tabS")
            for g in range(2):
                for hf in range(2):
                    for tt in range(NT):
                        zr = slice((1 - hf) * 64, (1 - hf) * 64 + 64)
                        P.add("dve", (lambda a=gkz[g][hf][tt].ap[zr]: lambda e: e.memset(a, 0.0))(), [], [gkz[g][hf][tt]])
            for g in range(2):
                slot, v = wload([(w_gkrep_d[l][:, g * 128:(g + 1) * 128], 128), (w_gkswrep_d[l][:, g * 128:(g + 1) * 128], 128)], 8)
                for tt in range(NT):
                    pz = bank("Z", [0, 1, 2])
                    pzs = bank("Zs", [3, 4, 5])
                    for kc in range(8):
                        mm(pz.ap, v[:, kc, 0:128], uT[kc][tt].ap, kc == 0, kc == 7, [slot, uT[kc][tt]], [pz])
                    for kc in range(8):
                        mm(pzs.ap, v[:, kc, 128:256], uT[kc][tt].ap, kc == 0, kc == 7, [slot, uT[kc][tt]], [pzs])
                    finalize_rope(pz, pzs, 128, blk_bf.ap[:, 0:128], 64, gx + 6, gx + 7, tt, gkz[g][0][tt], gkz[g][1][tt])
            for p in range(4):
                slot, v = wload([(wi[:, 672 + p * 128:672 + (p + 1) * 128], 128), (w_gqsw_d[l][:, p * 128:(p + 1) * 128], 128)], 8)
                for tt in range(NT):
                    pz = bank("Z", [0, 1, 2])
                    pzs = bank("Zs", [3, 4, 5])
                    for kc in range(8):
                        mm(pz.ap, v[:, kc, 0:128], uT[kc][tt].ap, kc == 0, kc == 7, [slot, uT[kc][tt]], [pz])
                    for kc in range(8):
                        mm(pzs.ap, v[:, kc, 128:256], uT[kc][tt].ap, kc == 0, kc == 7, [slot, uT[kc][tt]], [pzs])
                    finalize_rope(pz, pzs, 128, blk_bf.ap[:, 0:128], 64, gx + 4, gx + 5, tt, gq[p][tt])
            for c4 in range(4):
                for blkk in (0, 2, 4):
                    P.add("dve", (lambda a=gv[c4].ap[:, :, blkk, :]: lambda e: e.memset(a, 1.0))(), [], [gv[c4]])
            slot, v = wload([(wi[:, 1312:1440], 128)], 8)
            for c4 in range(4):
                pv = bank("Z", [0, 1, 2])
                for cc in range(4):
                    for kc in range(8):
                        mm(pv.ap[:, cc * 128:(cc + 1) * 128], uT[kc][c4].ap[:, cc * 128:(cc + 1) * 128], v[:, kc, 0:128],
                           kc == 0, kc == 7, [slot, uT[kc][c4]], [pv])
                pvv = pv.ap.rearrange("p (c g d) -> p c g d", c=4, g=2)
                for g in range(2):
                    dcopy(gv[c4].ap[:, :, 1 + 2 * g, :], pvv[:, :, g, :], [pv], [gv[c4]])
            if stop_stage == 31:
                return
            nb_g = nbias.ap[:, l * 2 + 1:l * 2 + 2]
            for p in range(4):
                g = p // 2
                for half in range(2):
                    prow = slice(half * 64, half * 64 + 64)
                    voff = (1 + 2 * g) * 64 if half == 0 else (2 * g) * 64

                    def v_of_chunk(c, voff=voff):
                        t = gv[c // 4]
                        fl = t.ap.rearrange("p c g d -> p c (g d)")
                        return t, fl[:, c % 4, voff:voff + 128]
                    attention(gkz[g][half], gq[p], slice(0, 128), 64, v_of_chunk, 64.0 ** -0.5, nb_g, og[p], half)
            if stop_stage == 32:
                return
            dma_sp(tabC.ap, tabs_d[2], [], [tabC], "tabC")
            dma_sp(tabS.ap, tabs_d[3], [], [tabS], "tabS")
            slA, vA = wload([(wi[:, 0:256], 256)], 8)
            slB, vB = wload([(wi[:, 256:512], 256)], 8)
            slC, vC = wload([(wi[:, 512:640], 128)], 8)
            srcs = [(slA, vA, 0), (slA, vA, 128), (slB, vB, 0), (slB, vB, 128), (slC, vC, 0)]
            for tt in range(NT):
                pzq = []
                for m in range(3):
                    sl, vv, c0 = srcs[m]
                    pz = bank("Zq", [0, 1, 2, 3, 4, 5])
                    pzq.append(pz)
                    for kc in range(8):
                        mm(pz.ap, vv[:, kc, c0:c0 + 128], uT[kc][tt].ap, kc == 0, kc == 7, [sl, uT[kc][tt]], [pz])
                sb = bank("stat", [6, 7])
                for m in range(3):
                    sq = ring("sq", sq_r)
                    act(sq.ap, pzq[m].ap, AF.Square, [pzq[m]], [sq])
                    mm(sb.ap, ones_bf.ap[:, 0:128], sq.ap, m == 0, m == 2, [ones_bf, sq], [sb])
                rs = rstd_from(sb.ap, 128, 384.0, [sb])
                for m in range(3):
                    stt(qn[m][tt].ap, pzq[m].ap, gc(GC_QA + l * 3 + m), rs.ap, ALU.mult, ALU.mult, [pzq[m], gcols, rs], [qn[m][tt]])
                pzk = []
                for m in range(2):
                    sl, vv, c0 = srcs[3 + m]
                    pz = bank("Zq", [0, 1, 2, 3, 4, 5])
                    pzk.append(pz)
                    for kc in range(8):
                        mm(pz.ap, vv[:, kc, c0:c0 + 128], uT[kc][tt].ap, kc == 0, kc == 7, [sl, uT[kc][tt]], [pz])
                sb = bank("stat", [6, 7])
                for m in range(2):
                    sq = ring("sq", sq_r)
                    act(sq.ap, pzk[m].ap, AF.Square, [pzk[m]], [sq])
                    mm(sb.ap, ones_bf.ap[:, 0:128], sq.ap, m == 0, m == 1, [ones_bf, sq], [sb])
                rs = rstd_from(sb.ap, 128, 256.0, [sb])
                for m in range(2):
                    stt(kvn[m][tt].ap, pzk[m].ap, gc(GC_KVA + l * 2 + m), rs.ap, ALU.mult, ALU.mult, [pzk[m], gcols, rs], [kvn[m][tt]])
            slot, v = wload([(w_krpad_d[l], 96), (w_krswpad_d[l], 96)], 8)
            kr = slice(64, 96)
            for tt in range(NT):
                pz = bank("Zq", [0, 1, 2, 3, 4, 5])
                pzs = bank("Zq", [0, 1, 2, 3, 4, 5])
                for kc in range(8):
                    mm(pz.ap[0:96], v[:, kc, 0:96], uT[kc][tt].ap, kc == 0, kc == 7, [slot, uT[kc][tt]], [pz])
                for kc in range(8):
                    mm(pzs.ap[0:96], v[:, kc, 96:192], uT[kc][tt].ap, kc == 0, kc == 7, [slot, uT[kc][tt]], [pzs])
                act(sqk[tt].ap, pz.ap[kr], AF.Square, [pz], [sqk[tt]])
                ta = ring("tA", tA_r)
                tb = ring("tB", tB_r)
                cs = slice(tt * TT, (tt + 1) * TT)
                stt(ta.ap[kr], pz.ap[kr], gc(gx + 2)[kr], tabC.ap[kr, cs], ALU.mult, ALU.mult, [pz, gcols, tabC], [ta])
                stt(tb.ap[kr], pzs.ap[kr], gc(gx + 3)[kr], tabS.ap[kr, cs], ALU.mult, ALU.mult, [pzs, gcols, tabS], [tb])
                tt_(gbk[tt].ap, ta.ap[kr], tb.ap[kr], ALU.add, [ta, tb], [gbk[tt]])
            if stop_stage == 33:
                return
            nb_m = nbias.ap[:, l * 2:l * 2 + 1]
            r96 = slice(0, 96)
            r64 = slice(0, 64)
            for tt in range(NT):
                P.add("dve", (lambda a=kThF[tt].ap[96:128]: lambda e: e.memset(a, 0.0))(), [], [kThF[tt]])
                P.add("dve", (lambda a=qThF[tt].ap[96:128]: lambda e: e.memset(a, 0.0))(), [], [qThF[tt]])
            for h in range(8):
                p, half = h // 2, h % 2
                s = wctr[0] % NWS
                wctr[0] += 1
                slot = wslot[s]
                vq = slot.ap[:, 0:288].rearrange("p (k c) -> p k c", k=3)
                vqs = slot.ap[:, 288:576].rearrange("p (k c) -> p k c", k=3)
                vkv = slot.ap[:, 576:832].rearrange("p (k c) -> p k c", k=2)
                dma_pool(vq, w_uq_d[l][:, h * 96:(h + 1) * 96].rearrange("(k p) c -> p k c", p=128), [], [slot], ("w", s))
                dma_pool(vqs, w_uqsw_d[l][:, h * 96:(h + 1) * 96].rearrange("(k p) c -> p k c", p=128), [], [slot], ("w", s))
                dma_pool(vkv, w_ukv_d[l][:, h * 128:(h + 1) * 128].rearrange("(k p) c -> p k c", p=128), [], [slot], ("w", s))
                pks = []
                for tt in range(NT):
                    pk = PS[tt]
                    pks.append(pk)
                    for kc in range(2):
                        mm(pk.ap[r64], vkv[:, kc, 0:64], kvn[kc][tt].ap, kc == 0, kc == 1, [slot, kvn[kc][tt]], [pk])
                for blkk in (0, 2):
                    P.add("dve", (lambda a=Vh.ap[:, :, blkk, :]: lambda e: e.memset(a, 1.0))(), [], [Vh])
                for c8 in range(2):
                    pv = PS[4 + c8]
                    for cc in range(8):
                        c = c8 * 8 + cc
                        for kc in range(2):
                            mm(pv.ap[:, cc * 64:(cc + 1) * 64], kvn[kc][c // 4].ap[:, (c % 4) * 128:(c % 4 + 1) * 128],
                               vkv[:, kc, 64:128], kc == 0, kc == 1, [slot, kvn[kc][c // 4]], [pv])
                    dcopy(Vh.ap[:, c8 * 8:(c8 + 1) * 8, 1, :], pv.ap.rearrange("p (c d) -> p c d", c=8), [pv], [Vh])
                for tt in range(NT):
                    pk = pks[tt]
                    P.add("dve", (lambda o=gbn[tt].ap, i_=pk.ap[r64], sc=gc(gx + 2)[r64]:
                                  lambda e: e.tensor_scalar(o, i_, sc, None, ALU.mult))(), [pk, gcols], [gbn[tt]])
                    act(sqn[tt].ap, pk.ap[r64], AF.Square, [pk], [sqn[tt]])
                for tt in range(NT):
                    sb = bank("stat", [6, 7])
                    sqfull = view(OFF_R1 + 28672 + tt * TT * 2, TT * 2, BF16)[r96]
                    mm(sb.ap[r96], ones_bf.ap[r96, 0:96], sqfull, True, True, [ones_bf, sqn[tt], sqk[tt]], [sb])
                    rs = rstd_from(sb.ap[r96], 96, 96.0, [sb])
                    gfull = view(OFF_R1 + 20480 + tt * TT * 4, TT * 4, F32)[r96]
                    tt_(kTh[tt].ap, gfull, rs.ap[r96], ALU.mult, [gbn[tt], gbk[tt], rs], [kTh[tt]])
                pqs_l = {}

                def q_mms(tt):
                    pq = PS[tt]
                    pqs = PS[4 + tt % 2]
                    pqs_l[tt] = (pq, pqs)
                    for kc in range(3):
                        mm(pq.ap[r96], vq[:, kc, :], qn[kc][tt].ap, kc == 0, kc == 2, [slot, qn[kc][tt]], [pq])
                    for kc in range(3):
                        mm(pqs.ap[r96], vqs[:, kc, :], qn[kc][tt].ap, kc == 0, kc == 2, [slot, qn[kc][tt]], [pqs])

                q_mms(0)
                q_mms(1)
                for tt in range(NT):
                    pq, pqs = pqs_l[tt]
                    finalize_rope(pq, pqs, 96, ones_bf.ap[r96, 0:96], 96.0, gx + 0, gx + 1, tt, qTh[tt])
                    if tt + 2 < NT:
                        q_mms(tt + 2)
                voff = 64 if half == 0 else 0

                def v_of_chunk(c, voff=voff):
                    fl = Vh.ap.rearrange("p c g d -> p c (g d)")
                    return Vh, fl[:, c, voff:voff + 128]
                attention(kThF, qThF, slice(0, 128), 96, v_of_chunk, 96.0 ** -0.5, nb_m, om[p], half)
            if stop_stage == 34:
                return
            def merge_norm(hf):
                norm_mod(l, b, 1, [hf * 2, hf * 2 + 1], upT)

            def merge_gates(hf):
                tts = [hf * 2, hf * 2 + 1]
                for dch in range(DC):
                    slG, vG = wload([(wi[:, 1440 + dch * 128:1440 + (dch + 1) * 128], 128),
                                     (wi[:, 2464 + dch * 128:2464 + (dch + 1) * 128], 128)], 8)
                    slB_, vBr = wload([(w_bm_d[l][:, dch * 128:(dch + 1) * 128], 128),
                                       (w_bg_d[l][:, dch * 128:(dch + 1) * 128], 128)], 4)
                    for t2 in range(2):
                        tt = tts[t2]
                        pg1 = bank("A", [0, 1])
                        pg2 = bank("B", [2, 3])
                        pb1 = bank("O", [4, 5])
                        pb2 = bank("stat", [6, 7])
                        for kc in range(8):
                            mm(pg1.ap, vG[:, kc, 0:128], upT[kc][t2].ap, kc == 0, kc == 7, [slG, upT[kc][t2]], [pg1])
                        for kc in range(8):
                            mm(pg2.ap, vG[:, kc, 128:256], upT[kc][t2].ap, kc == 0, kc == 7, [slG, upT[kc][t2]], [pg2])
                        for p in range(4):
                            ofull = view(OFF_UO + (p * T + tt * TT) * 2, TT * 2, BF16)
                            mm(pb1.ap, vBr[:, p, 0:128], ofull, p == 0, p == 3, [slB_, om[p][tt][0], om[p][tt][1]], [pb1])
                        for p in range(4):
                            ofull = view(OFF_OG + (p * T + tt * TT) * 2, TT * 2, BF16)
                            mm(pb2.ap, vBr[:, p, 128:256], ofull, p == 0, p == 3, [slB_, og[p][tt][0], og[p][tt][1]], [pb2])
                        ta = ring("tA", tA_r)
                        tb = ring("tB", tB_r)
                        act(ta.ap, pg1.ap, AF.Sigmoid, [pg1], [ta])
                        act(tb.ap, pg2.ap, AF.Sigmoid, [pg2], [tb])
                        tt_(ta.ap, ta.ap, pb1.ap, ALU.mult, [ta, pb1], [ta])
                        tt_(tb.ap, tb.ap, pb2.ap, ALU.mult, [tb, pb2], [tb])
                        tt_(mT[dch][t2].ap, ta.ap, tb.ap, ALU.add, [ta, tb], [mT[dch][t2]])

            def merge_out(hf):
                tts = [hf * 2, hf * 2 + 1]
                for dc in range(DC):
                    slO, vO = wload([(w_out_d[l][:, dc * 128:(dc + 1) * 128], 128)], 8)
                    for t2 in range(2):
                        tt = tts[t2]
                        po = bank("O", [4, 5])
                        for k in range(8):
                            mm(po.ap, vO[:, k, :], mT[k][t2].ap, k == 0, k == 7, [slO, mT[k][t2]], [po])
                        stt(hT[dc][tt].ap, po.ap, modcol(l, 5, dc, b), hT[dc][tt].ap, ALU.mult, ALU.add,
                            [po, modT, hT[dc][tt]], [hT[dc][tt]])

            merge_norm(0)
            merge_gates(0)
            merge_norm(1)
            merge_out(0)
            merge_gates(1)
            merge_out(1)

        for b in range(nseq):
            for ti in range(16):
                st = xs[ti % 2]
                dma_act(st.ap, x_d[b, ti * 128:(ti + 1) * 128, :], [], [st], ("xs", ti % 2))
                tt, k = ti // 4, ti % 4
                for hfb in range(2):
                    pb_ = bank("tr", [6, 7])
                    for d4 in range(4):
                        dc = hfb * 4 + d4
                        P.add("pe", (lambda o=pb_.ap[:, d4 * 128:(d4 + 1) * 128], i_=st.ap[:, dc * 128:(dc + 1) * 128]:
                                     lambda e: e.transpose(o, i_, ident.ap[:, 0:128]))(), [st, ident], [pb_])
                    hv = view(OFF_H, 65536, F32).rearrange("p (d t) -> p d t", d=DC)
                    dcopy(hv[:, hfb * 4:hfb * 4 + 4, tt * TT + k * 128:tt * TT + (k + 1) * 128],
                          pb_.ap.rearrange("p (d t) -> p d t", d=4), [pb_], [hT[hfb * 4 + d4][tt] for d4 in range(4)])
            for l in range(depth):
                for i in range(3):
                    stt(gsvv[:, i, :], modv[:, l, (3 * i + 1) * 8:(3 * i + 2) * 8, b], 1.0, n3v[:, l, i, :], ALU.add, ALU.mult,
                        [modT, norm3], [gsv])
                for hi_, i in ((0, 0), (1, 2)):
                    P.add("dve", (lambda o=hgvv[:, hi_, :], a=modv[:, l, (3 * i + 2) * 8:(3 * i + 3) * 8, b]:
                                  lambda e: e.tensor_scalar(o, a, 0.5, None, ALU.mult))(), [modT], [hgv])
                if stop_stage == 1:
                    break
                ffn(l, b, 0)
                if stop_stage == 2:
                    break
                mixer(l, b)
                if stop_stage == 3 or stop_stage > 30:
                    break
                ffn(l, b, 1)
            hv = view(OFF_H, 65536, F32).rearrange("p (d t) -> p d t", d=DC)
            for ti in range(16):
                st = ys[ti % 2]
                tt, k = ti // 4, ti % 4
                for hfb in range(2):
                    pb_ = bank("tr", [6, 7])
                    for d4 in range(4):
                        dc = hfb * 4 + d4
                        P.add("pe", (lambda o=pb_.ap[:, d4 * 128:(d4 + 1) * 128],
                                     i_=hT[dc][tt].ap[:, k * 128:(k + 1) * 128]:
                                     lambda e: e.transpose(o, i_, ident.ap[:, 0:128]))(), [hT[dc][tt], ident], [pb_])
                    dcopy(st.ap[:, hfb * 512:(hfb + 1) * 512], pb_.ap, [pb_], [st])
                dma_sp(out_d[b, ti * 128:(ti + 1) * 128, :], st.ap, [st], [], ("ys", ti % 2))
        P.add("sp", lambda e: e.nop(), [], [ys[0], ys[1]])
        P.emit(nc, es)
    return nc


def _rope_tables():
    f32 = np.float32

    def tables(pos, dim):
        inv = (np.float32(10000.0) ** (-(np.arange(0, dim, 2, dtype=f32)) / f32(dim))).astype(f32)
        ang = pos.astype(f32)[:, None] * inv[None, :]
        return np.cos(ang).astype(f32), np.sin(ang).astype(f32)

    t = np.arange(T)
    row = t // 64
    col = t % 64
    pc, ps = tables(t, 32)
    rc, rs = tables(row, 32)
    cc, cs = tables(col, 32)
    tabs = np.zeros((4, 128, T), f32)
    Cg = np.concatenate([rc.T, rc.T, cc.T, cc.T], axis=0)
    Sg = np.concatenate([-rs.T, rs.T, -cs.T, cs.T], axis=0)
    tabs[0] = np.concatenate([Cg, Cg], axis=0)
    tabs[1] = np.concatenate([Sg, Sg], axis=0)
    tabs[2, 0:64] = 1.0
    tabs[2, 64:96] = np.concatenate([pc.T, pc.T], axis=0)
    tabs[3, 64:96] = np.concatenate([-ps.T, ps.T], axis=0)
    return tabs


def _swap_idx(n, half):
    idx = np.arange(n)
    blk = idx // (2 * half)
    r = idx % (2 * half)
    return blk * 2 * half + (r + half) % (2 * half)


def _prep_shared(inp):
    f32 = np.float32
    A = lambda k: np.ascontiguousarray(np.asarray(inp[k], dtype=f32))
    w_in = A("w_in")
    sh = {}
    sh["w_ada"] = A("w_ada")
    sh["b_adaT"] = np.ascontiguousarray(A("b_ada").reshape(L, 72, 128).transpose(0, 2, 1))
    n3 = np.stack([A("norm_ffn1"), A("norm_mix"), A("norm_ffn2")], axis=1)
    sh["norm3"] = np.ascontiguousarray(n3.reshape(L, 3, 8, 128).transpose(3, 0, 1, 2).reshape(128, L * 24))
    for k in ("w_ffn1_in", "w_ffn1_out", "w_ffn2_in", "w_ffn2_out", "mla_w_uq", "mla_w_ukv", "w_branch_mla",
              "w_branch_gqa", "w_out"):
        sh[k] = A(k)
    sh["w_in"] = w_in
    sw16 = _swap_idx(512, 16)
    gq = w_in[:, :, 672:1184]
    sh["w_gq_sw"] = np.ascontiguousarray(gq[:, :, sw16])
    gkc = w_in[:, :, 1184:1312]
    gks = gkc[:, :, _swap_idx(128, 16)]
    rep = lambda a: np.ascontiguousarray(np.concatenate([a[:, :, 0:64], a[:, :, 0:64], a[:, :, 64:128], a[:, :, 64:128]], axis=2))
    sh["w_gk_rep"] = rep(gkc)
    sh["w_gk_sw_rep"] = rep(gks)
    kr = w_in[:, :, 640:672]
    z64 = np.zeros((L, D, 64), f32)
    sh["w_kr_pad"] = np.ascontiguousarray(np.concatenate([z64, kr], axis=2))
    sh["w_kr_sw_pad"] = np.ascontiguousarray(np.concatenate([z64, kr[:, :, _swap_idx(32, 16)]], axis=2))
    uq = A("mla_w_uq").reshape(L, 384, 8, 96)
    uqsw = np.zeros_like(uq)
    uqsw[:, :, :, 64:96] = uq[:, :, :, 64:96][:, :, :, _swap_idx(32, 16)]
    sh["w_uq_sw"] = np.ascontiguousarray(uqsw.reshape(L, 384, 768))
    NG = L * 3 + L * 2 + 8 * L
    gcols = np.zeros((128, NG), f32)
    qa = A("mla_q_a_norm")
    kva = A("mla_kv_a_norm")
    qkq = A("mla_qk_q_norm")
    qkk = A("mla_qk_k_norm")
    gqn = A("gqa_q_norm")
    gkn = A("gqa_k_norm")
    s32 = _swap_idx(32, 16)
    s64 = _swap_idx(64, 16)
    for l in range(L):
        gcols[:, l * 3:(l + 1) * 3] = qa[l].reshape(3, 128).T
        gcols[:, L * 3 + l * 2:L * 3 + (l + 1) * 2] = kva[l].reshape(2, 128).T
        gx = L * 3 + L * 2 + 8 * l
        gcols[0:96, gx + 0] = qkq[l]
        gcols[64:96, gx + 1] = qkq[l][64:96][s32]
        gcols[0:96, gx + 2] = qkk[l]
        gcols[64:96, gx + 3] = qkk[l][64:96][s32]
        gcols[:, gx + 4] = np.concatenate([gqn[l], gqn[l]])
        gcols[:, gx + 5] = np.concatenate([gqn[l][s64], gqn[l][s64]])
        gcols[:, gx + 6] = np.concatenate([gkn[l], gkn[l]])
        gcols[:, gx + 7] = np.concatenate([gkn[l][s64], gkn[l][s64]])
    sh["gcols"] = gcols
    grow = np.zeros((1, L * 4 * 96), f32)
    for l in range(L):
        grow[0, (l * 4 + 0) * 96:(l * 4 + 0) * 96 + 96] = qkq[l]
        grow[0, (l * 4 + 1) * 96:(l * 4 + 1) * 96 + 96] = qkk[l]
        grow[0, (l * 4 + 2) * 96:(l * 4 + 2) * 96 + 64] = gqn[l]
        grow[0, (l * 4 + 3) * 96:(l * 4 + 3) * 96 + 64] = gkn[l]
    sh["grow"] = grow
    sh["tabs"] = _rope_tables()
    sh["ident"] = np.eye(128, dtype=f32)
    return sh


_PROG_CACHE = {}


def _get_prog(nseq, depth=L, stop_stage=0):
    key = (nseq, depth, stop_stage)
    if key not in _PROG_CACHE:
        _PROG_CACHE[key] = build_program(nseq, depth, stop_stage)
    return _PROG_CACHE[key]


def kernel(**inputs):
    x = np.asarray(inputs["x"], dtype=np.float32)
    c = np.asarray(inputs["c"], dtype=np.float32)
    B = x.shape[0]
    nseq = B // NCORES
    sh = _prep_shared(inputs)
    in_maps = []
    for i in range(NCORES):
        m = dict(sh)
        m["x"] = np.ascontiguousarray(x[i * nseq:(i + 1) * nseq])
        m["cT"] = np.ascontiguousarray(c[i * nseq:(i + 1) * nseq].T)
        in_maps.append(m)
    nc = _get_prog(nseq)
    res = run_bass_kernel_spmd(nc, in_maps, core_ids=list(range(NCORES)))
    return np.concatenate([np.asarray(r["out"]) for r in res.results], axis=0).astype(np.float32)
```
